# Optimizing a Trainium2 kernel written in Bass

```python
import jax, jax.numpy as jnp
from jax import lax
import numpy as np

D_MODEL = 2048
BATCH = 2
SEQ = 8192
DEPTH = 2
DEC_BATCH = 8
DEC_SEQ = 64
PAST_LEN = 1024

CHUNK = 64
D_MIX = D_MODEL
D_A = 768
D_B = 768
D_C = 512
K_A = 3
K_B = 31
POOL_WINDOWS = (2, 4, 8, 16)
N_POOL_GROUPS = len(POOL_WINDOWS)
POOL_GROUP = D_C // N_POOL_GROUPS
POOL_PAD = max(POOL_WINDOWS) - 1
D_FF = 5632
D_IN = 3 * D_A + 2 * D_B + D_C
IN_SPLITS = tuple(np.cumsum([D_A, D_A, D_A, D_B, D_B])[:].tolist())
EPS = 1e-6

kernel_name = "hybrid_streaming_conv_pool_encoder_step"


def _rms(x, g):
    x32 = x.astype(jnp.float32)
    y = x32 * lax.rsqrt(jnp.mean(x32 * x32, axis=-1, keepdims=True) + EPS)
    return (y * g.astype(jnp.float32)).astype(x.dtype)


def _layernorm(x, g, b):
    x32 = x.astype(jnp.float32)
    mu = jnp.mean(x32, axis=-1, keepdims=True)
    xc = x32 - mu
    var = jnp.mean(xc * xc, axis=-1, keepdims=True)
    y = xc * lax.rsqrt(var + EPS) * g.astype(jnp.float32) + b.astype(jnp.float32)
    return y.astype(x.dtype)


def _swiglu(x, wg, wu, wd):
    return (jax.nn.silu(x @ wg) * (x @ wu)) @ wd


def _causal_dwconv(buf, u, w):
    k, c = w.shape
    full = jnp.concatenate([buf.astype(u.dtype), u], axis=1)
    y = lax.conv_general_dilated(full, w[:, None, :].astype(u.dtype), window_strides=(1,),
                                 padding='VALID', dimension_numbers=('NWC', 'WIO', 'NWC'),
                                 feature_group_count=c)
    return y, full[:, full.shape[1] - (k - 1):]


def _pool_mixer(buf, u, pos0, pool_w, pool_scale):
    bsz, seq_len, _ = u.shape
    full = jnp.concatenate([buf.astype(u.dtype), u], axis=1)
    f32 = full.astype(jnp.float32)
    cs = jnp.concatenate([jnp.zeros((bsz, 1, D_C), jnp.float32), lax.cumsum(f32, axis=1)], axis=1)
    pos = pos0 + jnp.arange(seq_len)
    end = cs[:, POOL_PAD + 1:POOL_PAD + 1 + seq_len]
    means = []
    for g, w in enumerate(POOL_WINDOWS):
        sl = slice(g * POOL_GROUP, (g + 1) * POOL_GROUP)
        s = end[:, :, sl] - cs[:, POOL_PAD + 1 - w:POOL_PAD + 1 - w + seq_len, sl]
        cnt = jnp.minimum(pos + 1, w).astype(jnp.float32)[None, :, None]
        means.append(s / cnt)
    d = (jnp.concatenate(means, axis=-1) - u.astype(jnp.float32)).astype(u.dtype)
    d = d.reshape(bsz, seq_len, N_POOL_GROUPS, POOL_GROUP)
    y = jnp.einsum('blgc,gcd->blgd', d, pool_w).reshape(bsz, seq_len, D_C) * pool_scale
    return y, full[:, full.shape[1] - POOL_PAD:]


def _mixer(xn, buf_a, buf_b, buf_p, pos0, w_in, conv_a_w, conv_b_w, conv_b_bias,
           ln_b_gain, ln_b_bias, pool_w, pool_scale, w_out):
    z = xn @ w_in
    h_a, b_a, c_a, glu_a, glu_g, u_p = jnp.split(z, IN_SPLITS, axis=-1)
    conv_a, nbuf_a = _causal_dwconv(buf_a, c_a * h_a, conv_a_w)
    y_a = b_a * conv_a
    v = glu_a * jax.nn.sigmoid(glu_g)
    conv_b, nbuf_b = _causal_dwconv(buf_b, v, conv_b_w)
    y_b = jax.nn.silu(_layernorm(conv_b + conv_b_bias, ln_b_gain, ln_b_bias))
    y_c, nbuf_p = _pool_mixer(buf_p, u_p, pos0, pool_w, pool_scale)
    out = jnp.concatenate([y_a, y_b, y_c], axis=-1) @ w_out
    return out, nbuf_a, nbuf_b, nbuf_p


def _trunk(x, bufs_a, bufs_b, bufs_p, pos0, ffn1_norm, ffn1_wg, ffn1_wu, ffn1_wd, mix_norm, w_in,
           conv_a_w, conv_b_w, conv_b_bias, ln_b_gain, ln_b_bias, pool_w, pool_scale, w_out,
           ffn2_norm, ffn2_wg, ffn2_wu, ffn2_wd, final_norm):
    h = x
    new_a, new_b, new_p = [], [], []
    for l in range(DEPTH):
        h = h + 0.5 * _swiglu(_rms(h, ffn1_norm[l]), ffn1_wg[l], ffn1_wu[l], ffn1_wd[l])
        m, na, nb, npool = _mixer(_rms(h, mix_norm[l]), bufs_a[l], bufs_b[l], bufs_p[l], pos0,
                                  w_in[l], conv_a_w[l], conv_b_w[l], conv_b_bias[l],
                                  ln_b_gain[l], ln_b_bias[l], pool_w[l], pool_scale[l], w_out[l])
        h = h + m
        h = h + 0.5 * _swiglu(_rms(h, ffn2_norm[l]), ffn2_wg[l], ffn2_wu[l], ffn2_wd[l])
        new_a.append(na)
        new_b.append(nb)
        new_p.append(npool)
    return _rms(h, final_norm), jnp.stack(new_a), jnp.stack(new_b), jnp.stack(new_p)


def setup_inputs(seed: int = 0) -> dict:
    key = jax.random.key(seed)
    ks = jax.random.split(key, 32)
    f = jnp.float32
    nrm = lambda k, shape, s: jax.random.normal(k, shape, f) * s
    return {
        "x_prompt": nrm(ks[0], (BATCH, SEQ, D_MODEL), 1.0),
        "x_sample": nrm(ks[1], (DEC_BATCH, DEC_SEQ, D_MODEL), 1.0),
        "cache_conv_a": nrm(ks[2], (DEPTH, DEC_BATCH, K_A - 1, D_A), 1.0),
        "cache_conv_b": nrm(ks[3], (DEPTH, DEC_BATCH, K_B - 1, D_B), 1.0),
        "cache_pool": nrm(ks[4], (DEPTH, DEC_BATCH, POOL_PAD, D_C), 1.0),
        "ffn1_norm": 1.0 + nrm(ks[5], (DEPTH, D_MODEL), 0.01),
        "ffn1_wg": nrm(ks[6], (DEPTH, D_MODEL, D_FF), D_MODEL ** -0.5),
        "ffn1_wu": nrm(ks[7], (DEPTH, D_MODEL, D_FF), D_MODEL ** -0.5),
        "ffn1_wd": nrm(ks[8], (DEPTH, D_FF, D_MODEL), D_FF ** -0.5),
        "mix_norm": 1.0 + nrm(ks[9], (DEPTH, D_MODEL), 0.01),
        "w_in": nrm(ks[10], (DEPTH, D_MODEL, D_IN), D_MODEL ** -0.5),
        "conv_a_w": nrm(ks[11], (DEPTH, K_A, D_A), K_A ** -0.5),
        "conv_b_w": nrm(ks[12], (DEPTH, K_B, D_B), K_B ** -0.5),
        "conv_b_bias": nrm(ks[13], (DEPTH, D_B), 0.02),
        "ln_b_gain": 1.0 + nrm(ks[14], (DEPTH, D_B), 0.01),
        "ln_b_bias": nrm(ks[15], (DEPTH, D_B), 0.02),
        "pool_w": nrm(ks[16], (DEPTH, N_POOL_GROUPS, POOL_GROUP, POOL_GROUP), POOL_GROUP ** -0.5),
        "pool_scale": 1.0 + nrm(ks[17], (DEPTH, D_C), 0.1),
        "w_out": nrm(ks[18], (DEPTH, D_MIX, D_MODEL), D_MIX ** -0.5),
        "ffn2_norm": 1.0 + nrm(ks[19], (DEPTH, D_MODEL), 0.01),
        "ffn2_wg": nrm(ks[20], (DEPTH, D_MODEL, D_FF), D_MODEL ** -0.5),
        "ffn2_wu": nrm(ks[21], (DEPTH, D_MODEL, D_FF), D_MODEL ** -0.5),
        "ffn2_wd": nrm(ks[22], (DEPTH, D_FF, D_MODEL), D_FF ** -0.5),
        "final_norm": 1.0 + nrm(ks[23], (D_MODEL,), 0.01),
    }


def reference(x_prompt, x_sample, cache_conv_a, cache_conv_b, cache_pool, ffn1_norm, ffn1_wg,
              ffn1_wu, ffn1_wd, mix_norm, w_in, conv_a_w, conv_b_w, conv_b_bias, ln_b_gain,
              ln_b_bias, pool_w, pool_scale, w_out, ffn2_norm, ffn2_wg, ffn2_wu, ffn2_wd, final_norm):
    weights = (ffn1_norm, ffn1_wg, ffn1_wu, ffn1_wd, mix_norm, w_in, conv_a_w, conv_b_w, conv_b_bias,
               ln_b_gain, ln_b_bias, pool_w, pool_scale, w_out, ffn2_norm, ffn2_wg, ffn2_wu, ffn2_wd,
               final_norm)
    bp = x_prompt.shape[0]
    dt = x_prompt.dtype
    zero_a = jnp.zeros((DEPTH, bp, K_A - 1, D_A), dt)
    zero_b = jnp.zeros((DEPTH, bp, K_B - 1, D_B), dt)
    zero_p = jnp.zeros((DEPTH, bp, POOL_PAD, D_C), dt)
    y_prompt, new_a_p, new_b_p, new_pool_p = _trunk(x_prompt, zero_a, zero_b, zero_p, 0, *weights)
    y_sample, new_a_s, new_b_s, new_pool_s = _trunk(x_sample, cache_conv_a, cache_conv_b, cache_pool,
                                                     PAST_LEN, *weights)
    return (y_prompt, y_sample, new_a_p, new_b_p, new_pool_p, new_a_s, new_b_s, new_pool_s)
```

```python
import numpy as np
from contextlib import ExitStack
import concourse.bass as bass
import concourse.mybir as mybir
from concourse.bass_utils import run_bass_kernel_spmd

F32 = mybir.dt.float32
BF16 = mybir.dt.bfloat16
ALU = mybir.AluOpType
AF = mybir.ActivationFunctionType

D = 2048
KC = 16
FF = 5632
NF = 44
GF = 2
NG = NF // GF
DIN = 4352
L = 2
TT = 544
SUB = 272
NP = 4
HALO = 64
SAMP = 64
HA, HB, HC = 2, 30, 15
CARW = 6 * HA + 6 * HB + 4 * HC
EPS = 1e-6
NCORES = 8
DBG = {}
SELF_SYNC_WINDOW = 6

_off = {}
_n = 0


def _alloc(name, n):
    global _n
    _off[name] = _n
    _n += n


for _l in range(L):
    _alloc(("n1", _l), KC)
    _alloc(("nm", _l), KC)
    _alloc(("n2", _l), KC)
    _alloc(("caw", _l), 6 * 3)
    _alloc(("cbw", _l), 6 * 31)
    _alloc(("cbb", _l), 6)
    _alloc(("lng", _l), 6)
    _alloc(("lnb", _l), 6)
    _alloc(("psc", _l), 4)
_alloc("nf", KC)
_alloc("hmask", 1)
_alloc("icnt", 4 * 16)
_alloc("cache", L * CARW)
NPRM = _n

ENGS = ("pe", "act", "dve", "pool", "sp")


class Op:
    __slots__ = ("eng", "fn", "reads", "writes", "dma", "key", "idx", "deps", "milestone", "mval")

    def __init__(self, eng, fn, reads, writes, dma, key):
        self.eng = eng
        self.fn = fn
        self.reads = reads
        self.writes = writes
        self.dma = dma
        self.key = key
        self.deps = None
        self.milestone = False
        self.mval = 0


class Prog:
    def __init__(self):
        self.ops = []

    def op(self, eng, fn, reads=(), writes=()):
        self.ops.append(Op(eng, fn, tuple(reads), tuple(writes), False, None))

    def dma(self, eng, fn, key, reads=(), writes=()):
        self.ops.append(Op(eng, fn, tuple(reads), tuple(writes), True, key))

    def analyze(self):
        last_w = {}
        readers = {}
        ops = self.ops
        epos = {}
        pos = [0] * len(ops)
        for i, o in enumerate(ops):
            pos[i] = epos.get(o.eng, 0)
            epos[o.eng] = pos[i] + 1
        for i, o in enumerate(ops):
            o.idx = i
            deps = set()
            for r in o.reads:
                w = last_w.get(r)
                if w is not None:
                    deps.add(w)
            for w_ in o.writes:
                w = last_w.get(w_)
                if w is not None:
                    deps.add(w)
                rs = readers.get(w_)
                if rs:
                    deps.update(rs)
            deps.discard(i)
            keep = set()
            for d in deps:
                p = ops[d]
                if p.dma:
                    if o.dma and o.key == p.key:
                        continue
                    keep.add(d)
                elif p.eng == o.eng and not o.dma:
                    if o.eng != "pe" and pos[i] - pos[d] <= SELF_SYNC_WINDOW:
                        keep.add(d)
                    continue
                else:
                    keep.add(d)
            o.deps = keep
            for w_ in o.writes:
                last_w[w_] = i
                readers[w_] = []
            for r in o.reads:
                readers.setdefault(r, []).append(i)
        for o in ops:
            for d in o.deps:
                ops[d].milestone = True
        cnt = {}
        for o in ops:
            if o.dma:
                k = ("dma", o.key)
                cnt[k] = cnt.get(k, 0) + 16
                o.mval = cnt[k]
            elif o.milestone:
                k = ("eng", o.eng)
                cnt[k] = cnt.get(k, 0) + 1
                o.mval = cnt[k]
        self.sem_final = cnt

    def emit(self, nc, final_waits=()):
        self.analyze()
        ops = self.ops
        with ExitStack() as es:
            sems = {}
            for k in self.sem_final:
                nm = "s_" + "_".join(str(x) for x in (k[1] if isinstance(k[1], tuple) else (k[1],)))
                sems[k] = es.enter_context(nc.semaphore(nm))
            block = es.enter_context(nc.Block())
            per_eng = {e: [] for e in ENGS}
            for o in ops:
                per_eng[o.eng].append(o)

            def run(engine_obj, ename):
                waited = {}
                for o in per_eng[ename]:
                    need = {}
                    for d in o.deps:
                        p = ops[d]
                        k = ("dma", p.key) if p.dma else ("eng", p.eng)
                        if p.mval > need.get(k, 0):
                            need[k] = p.mval
                    for k, v in need.items():
                        if waited.get(k, 0) >= v:
                            continue
                        engine_obj.wait_ge(sems[k], v)
                        waited[k] = v
                    ins = o.fn(engine_obj)
                    if o.dma:
                        ins.then_inc(sems[("dma", o.key)], 16)
                    elif o.milestone:
                        ins.then_inc(sems[("eng", o.eng)], 1)
                if ename == "sp":
                    for key in final_waits:
                        k = ("dma", key)
                        if k in sems:
                            engine_obj.wait_ge(sems[k], self.sem_final[k])

            @block.tensor
            def _(e):
                run(e, "pe")

            @block.scalar
            def _(e):
                run(e, "act")

            @block.vector
            def _(e):
                run(e, "dve")

            @block.gpsimd
            def _(e):
                run(e, "pool")

            @block.sync
            def _(e):
                run(e, "sp")


class Rot:
    def __init__(self, tiles, name):
        self.tiles = tiles
        self.name = name
        self.i = 0

    def next(self):
        k = self.i % len(self.tiles)
        self.i += 1
        return self.tiles[k], (self.name, k)


def segs_of(p):
    if p == 0:
        return [(1, 0, SAMP), (0, SAMP, TT - SAMP)]
    return [(0, 0, TT)]


def seg_off(p, H):
    offs = []
    o = 0
    for (_, _, n) in segs_of(p):
        offs.append(o + H)
        o += H + n
    return offs


def pieces(p):
    out = []
    for s in range(2):
        a, b = s * SUB, (s + 1) * SUB
        for si, (_, c0, n) in enumerate(segs_of(p)):
            lo, hi = max(a, c0), min(b, c0 + n)
            if lo < hi:
                out.append((s, lo, hi, si))
    return out


def build_nc():
    nc = bass.Bass("TRN2", target_bir_lowering=False)
    dt = lambda name, shape, kind: nc.dram_tensor(name, shape, F32, kind=kind).ap()
    xT = dt("xT", [128, NP, KC, TT], "ExternalInput")
    prm_d = dt("prm", [128, NPRM], "ExternalInput")
    w_d = {}
    for nm, shp in (("ffn1_wg", [L, D, FF]), ("ffn1_wu", [L, D, FF]), ("ffn1_wd", [L, FF, D]),
                    ("w_in", [L, D, DIN]), ("pool_w", [L, 4, 128, 128]), ("w_out", [L, D, D]),
                    ("ffn2_wg", [L, D, FF]), ("ffn2_wu", [L, D, FF]), ("ffn2_wd", [L, FF, D])):
        w_d[nm] = dt(nm, shp, "ExternalInput")
    yT = dt("yT", [128, NP, KC, TT], "ExternalOutput")
    nb_d = dt("nbuf", [128, 2 * L * CARW], "ExternalOutput")

    P = Prog()
    with ExitStack() as es:
        sb = lambda n, s, d=F32: es.enter_context(nc.sbuf_tensor(n, s, d))
        resid = sb("resid", [128, KC, TT])
        xn = sb("xn", [128, KC, TT], BF16)
        ycat = sb("ycat", [128, KC, TT], BF16)
        convb = sb("convb", [128, 6, TT])
        hid = [sb(f"hid{i}", [128, GF, TT], BF16) for i in range(2)]
        halves = [sb(f"wh{i}", [128, KC, 256], BF16) for i in range(4)]
        wds = [sb(f"wd{i}", [128, GF, D], BF16) for i in range(2)]
        PADW = TT + 2 * HB
        sqb = Rot([sb(f"sqb{i}", [128, TT]) for i in range(3)], "sqb")
        sgb = Rot([sb(f"sg{i}", [128, SUB]) for i in range(4)], "sg")
        zev = Rot([sb(f"zev{i}", [128, TT]) for i in range(6)], "zev")
        padb = Rot([sb(f"pad{i}", [128, PADW]) for i in range(3)], "pad")
        ptmp = Rot([sb(f"ptmp{i}", [128, PADW]) for i in range(2)], "ptmp")
        dbf = Rot([sb(f"dbf{i}", [128, TT], BF16) for i in range(2)], "dbf")
        rstd = sb("rstd", [128, TT])
        sqacc = [sb(f"sqacc{i}", [128, TT]) for i in range(4)]
        lnm = sb("lnm", [128, TT])
        lnr = sb("lnr", [128, TT])
        lnt = sb("lnt", [128, TT])
        prm = sb("prm_sb", [128, NPRM])
        car = sb("car", [128, 2 * L * CARW])
        pw_sb = sb("pw_sb", [128, L * 4, 128], BF16)
        ones = sb("ones", [128, 128])
        epst = sb("epst", [128, 1])
        pb = [es.enter_context(nc.psum_tensor(f"pb{i}", [128, 512], F32)) for i in range(8)]
        guB = Rot(list(range(0, 4)), "guB")
        dnB = Rot(list(range(4, 8)), "dnB")
        zB = Rot(list(range(0, 6)), "zB")
        allB = Rot(list(range(0, 8)), "allB")
        xB = Rot(list(range(6, 8)), "xB")
        hctr = [0]
        wdctr = [0]

        def pcol(name, i=0):
            o = _off[name] + i
            return prm[:, o:o + 1]

        def car_sl(kind, l, o, n):
            base = (kind * L + l) * CARW + o
            return car[:, base:base + n]

        P.dma("sp", lambda e: e.dma_start(out=prm[:], in_=prm_d), "prm", writes=["prm"])
        P.dma("pool", lambda e: e.dma_start(out=pw_sb[:], in_=w_d["pool_w"].rearrange("l g c d -> c (l g) d")),
              "prm2", writes=["pw"])
        P.op("dve", lambda e: e.memset(ones[:], 1.0), writes=["ones"])
        P.op("dve", lambda e: e.memset(epst[:], EPS), writes=["epst"])
        P.op("dve", lambda e: e.memset(car[:, 0:L * CARW], 0.0), writes=["car"])
        co = _off["cache"]
        P.op("dve", lambda e: e.tensor_copy(out=car[:, L * CARW:2 * L * CARW], in_=prm[:, co:co + L * CARW]),
             reads=["prm"], writes=["car"])

        def load_x(p):
            for q in range(8):
                P.dma("sp", lambda e, q=q: e.dma_start(out=resid[:, 2 * q:2 * q + 2, :], in_=xT[:, p, 2 * q:2 * q + 2, :]),
                      ("x", q), writes=[("r", kc, s) for kc in range(2 * q, 2 * q + 2) for s in range(2)])

        def sumsq_rstd(src_fn, nchunks, src_res_fn, out_tile, out_res, scale, mean_tile=None):
            b0, b1 = xB.next()[0], xB.next()[0]
            bs = (b0, b1)
            if mean_tile is not None:
                ms = (4, 5)
            for c in range(nchunks):
                sq, sqr = sqb.next()
                P.op("act", lambda e, c=c, sq=sq: e.activation(out=sq[:], in_=src_fn(c), func=AF.Square),
                     reads=src_res_fn(c), writes=[sqr])
                for s in range(2):
                    P.op("pe", lambda e, c=c, s=s, sq=sq: e.matmul(pb[bs[s]][:, 0:SUB], lhsT=ones[:], rhs=sq[:, s * SUB:(s + 1) * SUB],
                                                                  start=(c == 0), stop=(c == nchunks - 1)),
                         reads=["ones", sqr], writes=[("pb", bs[s])])
                    if mean_tile is not None:
                        P.op("pe", lambda e, c=c, s=s: e.matmul(pb[ms[s]][:, 0:SUB], lhsT=ones[:], rhs=src_fn(c)[:, s * SUB:(s + 1) * SUB],
                                                               start=(c == 0), stop=(c == nchunks - 1)),
                             reads=["ones"] + src_res_fn(c), writes=[("pb", ms[s])])
            for s in range(2):
                cs = slice(s * SUB, (s + 1) * SUB)
                if mean_tile is None:
                    P.op("act", lambda e, s=s, cs=cs: e.activation(out=out_tile[:, cs], in_=pb[bs[s]][:, 0:SUB], func=AF.Sqrt,
                                                                  scale=scale, bias=epst[:, 0:1]),
                         reads=["epst"], writes=[("pb", bs[s]), (out_res, s)])
                else:
                    P.op("act", lambda e, s=s, cs=cs: e.activation(out=mean_tile[:, cs], in_=pb[ms[s]][:, 0:SUB], func=AF.Identity,
                                                                  scale=scale),
                         writes=[("pb", ms[s]), ("lnm", s)])
                    P.op("dve", lambda e, s=s, cs=cs: e.tensor_tensor(out=lnt[:, cs], in0=mean_tile[:, cs], in1=mean_tile[:, cs], op=ALU.mult),
                         reads=[("lnm", s)], writes=[("lnt", s)])
                    P.op("dve", lambda e, s=s, cs=cs: e.scalar_tensor_tensor(out=lnt[:, cs], in0=pb[bs[s]][:, 0:SUB], scalar=scale,
                                                                            in1=lnt[:, cs], op0=ALU.mult, op1=ALU.subtract),
                         writes=[("pb", bs[s]), ("lnt", s)])
                    P.op("act", lambda e, s=s, cs=cs: e.activation(out=out_tile[:, cs], in_=lnt[:, cs], func=AF.Sqrt,
                                                                  scale=1.0, bias=epst[:, 0:1]),
                         reads=["epst", ("lnt", s)], writes=[(out_res, s)])
                P.op("dve", lambda e, s=s, cs=cs: e.reciprocal(out=out_tile[:, cs], in_=out_tile[:, cs]),
                     writes=[(out_res, s)])

        pending_adds = []

        def rms_sq(kc, lag=0):
            q, r = kc // 4, kc % 4
            if r == 0:
                P.op("act", lambda e: e.activation(out=sqacc[q][:], in_=resid[:, kc, :], func=AF.Square),
                     reads=[("r", kc, 0), ("r", kc, 1)], writes=[("sqacc", q)])
            else:
                sq, sqr = sqb.next()
                P.op("act", lambda e: e.activation(out=sq[:], in_=resid[:, kc, :], func=AF.Square),
                     reads=[("r", kc, 0), ("r", kc, 1)], writes=[sqr])
                pending_adds.append((q, sq, sqr))
            while len(pending_adds) > lag:
                q2, sq2, sqr2 = pending_adds.pop(0)
                P.op("dve", lambda e, q2=q2, sq2=sq2: e.tensor_tensor(out=sqacc[q2][:], in0=sqacc[q2][:], in1=sq2[:], op=ALU.add),
                     reads=[sqr2], writes=[("sqacc", q2)])

        def rms_flush():
            while pending_adds:
                q2, sq2, sqr2 = pending_adds.pop(0)
                P.op("dve", lambda e, q2=q2, sq2=sq2: e.tensor_tensor(out=sqacc[q2][:], in0=sqacc[q2][:], in1=sq2[:], op=ALU.add),
                     reads=[sqr2], writes=[("sqacc", q2)])

        def rms_finish():
            rms_flush()
            bs = (xB.next()[0], xB.next()[0])
            for q in range(4):
                for s in range(2):
                    P.op("pe", lambda e, q=q, s=s: e.matmul(pb[bs[s]][:, 0:SUB], lhsT=ones[:], rhs=sqacc[q][:, s * SUB:(s + 1) * SUB],
                                                           start=(q == 0), stop=(q == 3)),
                         reads=["ones", ("sqacc", q)], writes=[("pb", bs[s])])
            for s in range(2):
                cs = slice(s * SUB, (s + 1) * SUB)
                P.op("act", lambda e, s=s, cs=cs: e.activation(out=rstd[:, cs], in_=pb[bs[s]][:, 0:SUB], func=AF.Sqrt,
                                                              scale=1.0 / D, bias=epst[:, 0:1]),
                     reads=["epst"], writes=[("pb", bs[s]), ("rstd", s)])
                P.op("dve", lambda e, s=s, cs=cs: e.reciprocal(out=rstd[:, cs], in_=rstd[:, cs]),
                     writes=[("rstd", s)])

        def rmsnorm(gname):
            rms_finish()
            for kc in range(KC):
                P.op("dve", lambda e, kc=kc: e.scalar_tensor_tensor(out=xn[:, kc, :], in0=resid[:, kc, :], scalar=pcol(gname, kc),
                                                                   in1=rstd[:], op0=ALU.mult, op1=ALU.mult),
                     reads=[("r", kc, 0), ("r", kc, 1), ("rstd", 0), ("rstd", 1), "prm"], writes=[("xn", kc)])

        def final_norm_store(p, last):
            rms_finish()
            stage = ([(sqb.tiles[i], ("sqb", i)) for i in range(3)] + [(zev.tiles[i], ("zev", i)) for i in range(6)])
            for kc in range(KC):
                yt, ytr = stage[kc % len(stage)]
                P.op("dve", lambda e, kc=kc, yt=yt: e.scalar_tensor_tensor(out=yt[:], in0=resid[:, kc, :], scalar=pcol("nf", kc),
                                                                          in1=rstd[:], op0=ALU.mult, op1=ALU.mult),
                     reads=[("r", kc, 0), ("r", kc, 1), ("rstd", 0), ("rstd", 1), "prm"], writes=[ytr])
                P.dma("sp", lambda e, kc=kc, yt=yt: e.dma_start(out=yT[:, p, kc, :], in_=yt[:]), ("y",) + ytr, reads=[ytr])

        def ffn(l, wg, wu, wd):
            wgv = wg[l].rearrange("(kc p) f -> p kc f", p=128)
            wuv = wu[l].rearrange("(kc p) f -> p kc f", p=128)
            wdv = wd[l].rearrange("(fc p) d -> p fc d", p=128)
            slots = {}

            def dn_tile(gg, d, s, rot=None):
                hg, hu, wdi = slots[gg]
                b = (rot or dnB).next()[0]
                cs = slice(s * SUB, (s + 1) * SUB)
                for fi in range(GF):
                    P.op("pe", lambda e, fi=fi: e.matmul(pb[b][:, 0:SUB], lhsT=wds[wdi][:, fi, d * 128:(d + 1) * 128],
                                                        rhs=hid[gg % 2][:, fi, cs], start=(fi == 0), stop=(fi == GF - 1)),
                         reads=[("wd", wdi), ("hd", gg % 2, fi, s)], writes=[("pb", b)])
                P.op("dve", lambda e: e.scalar_tensor_tensor(out=resid[:, d, cs], in0=pb[b][:, 0:SUB], scalar=0.5,
                                                            in1=resid[:, d, cs], op0=ALU.mult, op1=ALU.add),
                     writes=[("pb", b), ("r", d, s)])

            for g in range(NG + 1):
                gu_tiles = []
                if g < NG:
                    hg = hctr[0] % 4
                    hu = (hctr[0] + 1) % 4
                    hctr[0] += 2
                    wdi = wdctr[0] % 2
                    wdctr[0] += 1
                    slots[g] = (hg, hu, wdi)
                    f0 = g * GF * 128
                    P.dma("pool", lambda e, hg=hg, f0=f0: e.dma_start(out=halves[hg][:], in_=wgv[:, :, f0:f0 + 256]),
                          ("h", hg), writes=[("h", hg)])
                    P.dma("pool", lambda e, hu=hu, f0=f0: e.dma_start(out=halves[hu][:], in_=wuv[:, :, f0:f0 + 256]),
                          ("h", hu), writes=[("h", hu)])
                    P.dma("pool", lambda e, wdi=wdi, g=g: e.dma_start(out=wds[wdi][:], in_=wdv[:, g * GF:(g + 1) * GF, :]),
                          ("wd", wdi), writes=[("wd", wdi)])
                    for fi in range(GF):
                        for gu in range(2):
                            for s in range(2):
                                gu_tiles.append((fi, gu, s))
                dn_tiles = [(d, s) for d in range(KC) for s in range(2)] if g >= 1 else []
                sgmap = {}
                nsteps = max(len(gu_tiles), 8)
                pre_banks = {}
                if g == 0:
                    for i in range(4):
                        pre_banks[i] = guB.next()[0]
                    for kc in range(KC):
                        for i in range(4):
                            fi, gu, s = gu_tiles[i]
                            hsel = hg if gu == 0 else hu
                            b = pre_banks[i]
                            cs = slice(s * SUB, (s + 1) * SUB)
                            P.op("pe", lambda e, kc=kc, hsel=hsel, fi=fi, cs=cs, b=b: e.matmul(
                                pb[b][:, 0:SUB], lhsT=halves[hsel][:, kc, fi * 128:(fi + 1) * 128], rhs=xn[:, kc, cs],
                                start=(kc == 0), stop=(kc == KC - 1)),
                                reads=[("h", hsel), ("xn", kc)], writes=[("pb", b)])
                for i in range(nsteps):
                    if i < len(gu_tiles):
                        fi, gu, s = gu_tiles[i]
                        hsel = hg if gu == 0 else hu
                        cs = slice(s * SUB, (s + 1) * SUB)
                        if i in pre_banks:
                            b = pre_banks[i]
                        else:
                            b = guB.next()[0]
                            for kc in range(KC):
                                P.op("pe", lambda e, kc=kc, hsel=hsel, fi=fi, cs=cs, b=b: e.matmul(
                                    pb[b][:, 0:SUB], lhsT=halves[hsel][:, kc, fi * 128:(fi + 1) * 128], rhs=xn[:, kc, cs],
                                    start=(kc == 0), stop=(kc == KC - 1)),
                                    reads=[("h", hsel), ("xn", kc)], writes=[("pb", b)])
                        if gu == 0:
                            sg, sgr = sgb.next()
                            sgmap[(fi, s)] = (sg, sgr)
                            P.op("act", lambda e, sg=sg, b=b: e.activation(out=sg[:], in_=pb[b][:, 0:SUB], func=AF.Silu),
                                 writes=[("pb", b), sgr])
                        else:
                            sg, sgr = sgmap[(fi, s)]
                            P.op("dve", lambda e, sg=sg, b=b, fi=fi, cs=cs, g=g: e.tensor_tensor(
                                out=hid[g % 2][:, fi, cs], in0=pb[b][:, 0:SUB], in1=sg[:], op=ALU.mult),
                                reads=[sgr], writes=[("pb", b), ("hd", g % 2, fi, s)])
                    if dn_tiles:
                        for (d, s) in dn_tiles[4 * i:4 * i + 4]:
                            dn_tile(g - 1, d, s, allB if g == NG else None)
                            if g == NG and s == 1:
                                rms_sq(d, lag=2)

        def mixer(p, l):
            segs = segs_of(p)
            pcs = pieces(p)
            winv = w_d["w_in"][l].rearrange("(kc p) f -> p kc f", p=128)
            wov = w_d["w_out"][l].rearrange("(kc p) f -> p kc f", p=128)
            Z = []
            for j in range(6):
                Z.append(("gg", j, 3072 + j * 128))
                Z.append(("ga", j, 2304 + j * 128))
                Z.append(("ha", j, 0 + j * 128))
                Z.append(("ca", j, 1536 + j * 128))
                Z.append(("ba", j, 768 + j * 128))
            Z = [("up", g, 3840 + g * 128) for g in range(4)] + Z
            held = {}
            pend_pe = []

            def masked_prefix(buf, bufr, H, caro, l):
                offs = seg_off(p, H)
                if p == 0:
                    o1 = offs[1]
                    P.op("dve", lambda e: e.tensor_scalar(out=buf[:, o1:o1 + HALO], in0=buf[:, o1:o1 + HALO],
                                                         scalar1=pcol("hmask"), scalar2=None, op0=ALU.mult),
                         reads=["prm"], writes=[bufr])
                for si, (kind, c0, n) in enumerate(segs):
                    o = offs[si]
                    P.op("dve", lambda e, o=o, kind=kind: e.tensor_copy(out=buf[:, o - H:o], in_=car_sl(kind, l, caro, H)),
                         reads=["car"], writes=[bufr])

            def carry_update(buf, bufr, H, caro, l):
                offs = seg_off(p, H)
                for si, (kind, c0, n) in enumerate(segs):
                    o = offs[si]
                    P.op("dve", lambda e, o=o, n=n, kind=kind: e.tensor_copy(out=car_sl(kind, l, caro, H), in_=buf[:, o + n - H:o + n]),
                         reads=[bufr], writes=["car"])

            for zi, (typ, j, col) in enumerate(Z):
                if zi % 2 == 0:
                    h = hctr[0] % 4
                    hctr[0] += 1
                    cur_h = h
                    for t in range(2):
                        if zi + t < len(Z):
                            c2 = Z[zi + t][2]
                            P.dma("pool", lambda e, h=h, t=t, c2=c2: e.dma_start(out=halves[h][:, :, t * 128:(t + 1) * 128],
                                                                                 in_=winv[:, :, c2:c2 + 128]),
                                  ("h", h), writes=[("h", h)])
                hh = cur_h
                t = zi % 2
                if zi == 0:
                    pre_banks = {0: [zB.next()[0], zB.next()[0]], 1: [zB.next()[0], zB.next()[0]]}
                    for kc in range(KC):
                        for t2 in range(2):
                            for s in range(2):
                                b = pre_banks[t2][s]
                                cs = slice(s * SUB, (s + 1) * SUB)
                                P.op("pe", lambda e, kc=kc, hh=hh, t2=t2, cs=cs, b=b: e.matmul(
                                    pb[b][:, 0:SUB], lhsT=halves[hh][:, kc, t2 * 128:(t2 + 1) * 128], rhs=xn[:, kc, cs],
                                    start=(kc == 0), stop=(kc == KC - 1)),
                                    reads=[("h", hh), ("xn", kc)], writes=[("pb", b)])
                if zi < 2:
                    banks = pre_banks[zi]
                else:
                    banks = []
                    for s in range(2):
                        b = zB.next()[0]
                        banks.append(b)
                        cs = slice(s * SUB, (s + 1) * SUB)
                        for kc in range(KC):
                            P.op("pe", lambda e, kc=kc, hh=hh, t=t, cs=cs, b=b: e.matmul(
                                pb[b][:, 0:SUB], lhsT=halves[hh][:, kc, t * 128:(t + 1) * 128], rhs=xn[:, kc, cs],
                                start=(kc == 0), stop=(kc == KC - 1)),
                                reads=[("h", hh), ("xn", kc)], writes=[("pb", b)])
                while pend_pe:
                    pend_pe.pop(0)()
                if typ in ("gg", "ga", "ha", "ca", "ba"):
                    zt, ztr = zev.next()
                    fn = AF.Sigmoid if typ == "gg" else AF.Identity
                    for s in range(2):
                        cs = slice(s * SUB, (s + 1) * SUB)
                        P.op("act", lambda e, zt=zt, cs=cs, b=banks[s], fn=fn: e.activation(out=zt[:, cs], in_=pb[b][:, 0:SUB], func=fn),
                             writes=[("pb", banks[s]), ztr])
                    held[typ] = (zt, ztr)
                if typ == "ga":
                    sg_t, sg_r = held["gg"]
                    ga_t, ga_r = held["ga"]
                    vf, vfr = padb.next()
                    offs = seg_off(p, HB)
                    for si, (kind, c0, n) in enumerate(segs):
                        o = offs[si]
                        P.op("dve", lambda e, o=o, c0=c0, n=n, vf=vf, ga_t=ga_t, sg_t=sg_t: e.tensor_tensor(
                            out=vf[:, o:o + n], in0=ga_t[:, c0:c0 + n], in1=sg_t[:, c0:c0 + n], op=ALU.mult),
                            reads=[sg_r, ga_r], writes=[vfr])
                    masked_prefix(vf, vfr, HB, 12 + j * HB, l)
                    carry_update(vf, vfr, HB, 12 + j * HB, l)
                    wb = _off[("cbw", l)] + j * 31
                    NACC = 4
                    for si, (kind, c0, n) in enumerate(segs):
                        o = offs[si] - HB
                        accs = [(lambda c0=c0, n=n, j=j: convb[:, j, c0:c0 + n], [("cb", j)]),
                                (lambda c0=c0, n=n: lnm[:, c0:c0 + n], [("lnm", 0), ("lnm", 1)]),
                                (lambda c0=c0, n=n: lnt[:, c0:c0 + n], [("lnt", 0), ("lnt", 1)]),
                                (lambda c0=c0, n=n: lnr[:, c0:c0 + n], [("lnr", 0), ("lnr", 1)])]
                        for k in range(31):
                            afn, ares = accs[k % NACC]
                            if k == 0:
                                P.op("act", lambda e, o=o, n=n, vf=vf, wb=wb, afn=afn, j=j: e.activation(
                                    out=afn(), in_=vf[:, o:o + n], func=AF.Identity, scale=prm[:, wb:wb + 1],
                                    bias=pcol(("cbb", l), j)),
                                    reads=[vfr, "prm"], writes=ares)
                            elif k < NACC:
                                P.op("act", lambda e, o=o, n=n, vf=vf, wb=wb, afn=afn, k=k: e.activation(
                                    out=afn(), in_=vf[:, o + k:o + k + n], func=AF.Identity, scale=prm[:, wb + k:wb + k + 1]),
                                    reads=[vfr, "prm"], writes=ares)
                            else:
                                P.op("dve", lambda e, o=o, n=n, vf=vf, wb=wb, afn=afn, k=k: e.scalar_tensor_tensor(
                                    out=afn(), in0=vf[:, o + k:o + k + n], scalar=prm[:, wb + k:wb + k + 1],
                                    in1=afn(), op0=ALU.mult, op1=ALU.add),
                                    reads=[vfr, "prm"], writes=ares)
                        P.op("dve", lambda e, a0=accs[0][0], a1=accs[1][0]: e.tensor_tensor(out=a0(), in0=a0(), in1=a1(), op=ALU.add),
                             reads=accs[1][1], writes=accs[0][1])
                        P.op("dve", lambda e, a2=accs[2][0], a3=accs[3][0]: e.tensor_tensor(out=a2(), in0=a2(), in1=a3(), op=ALU.add),
                             reads=accs[3][1], writes=accs[2][1])
                        P.op("dve", lambda e, a0=accs[0][0], a2=accs[2][0]: e.tensor_tensor(out=a0(), in0=a0(), in1=a2(), op=ALU.add),
                             reads=accs[2][1], writes=accs[0][1])
                elif typ == "ca":
                    ha_t, ha_r = held["ha"]
                    ca_t, ca_r = held["ca"]
                    cf, cfr = padb.next()
                    offs = seg_off(p, HA)
                    for si, (kind, c0, n) in enumerate(segs):
                        o = offs[si]
                        P.op("dve", lambda e, o=o, c0=c0, n=n, cf=cf, ca_t=ca_t, ha_t=ha_t: e.tensor_tensor(
                            out=cf[:, o:o + n], in0=ca_t[:, c0:c0 + n], in1=ha_t[:, c0:c0 + n], op=ALU.mult),
                            reads=[ha_r, ca_r], writes=[cfr])
                    masked_prefix(cf, cfr, HA, j * HA, l)
                    cva, cvar = zev.next()
                    wb = _off[("caw", l)] + j * 3
                    for si, (kind, c0, n) in enumerate(segs):
                        o = offs[si] - HA
                        P.op("dve", lambda e, o=o, c0=c0, n=n, cf=cf, cva=cva, wb=wb: e.tensor_scalar(
                            out=cva[:, c0:c0 + n], in0=cf[:, o:o + n], scalar1=prm[:, wb:wb + 1], scalar2=None, op0=ALU.mult),
                            reads=[cfr, "prm"], writes=[cvar])
                        for k in range(1, 3):
                            P.op("dve", lambda e, o=o, c0=c0, n=n, cf=cf, cva=cva, wb=wb, k=k: e.scalar_tensor_tensor(
                                out=cva[:, c0:c0 + n], in0=cf[:, o + k:o + k + n], scalar=prm[:, wb + k:wb + k + 1],
                                in1=cva[:, c0:c0 + n], op0=ALU.mult, op1=ALU.add),
                                reads=[cfr, "prm"], writes=[cvar])
                    carry_update(cf, cfr, HA, j * HA, l)
                    held["cva"] = (cva, cvar)
                elif typ == "ba":
                    ba_t, ba_r = held["ba"]
                    cva, cvar = held["cva"]
                    P.op("dve", lambda e, j=j, ba_t=ba_t, cva=cva: e.tensor_tensor(out=ycat[:, j, :], in0=ba_t[:], in1=cva[:], op=ALU.mult),
                         reads=[ba_r, cvar], writes=[("yc", j)])
                elif typ == "up":
                    g = j
                    w = 2 ** (g + 1)
                    uf, ufr = padb.next()
                    offs = seg_off(p, HC)
                    for (s, lo, hi, si) in pcs:
                        o = offs[si] + (lo - segs[si][1])
                        P.op("act", lambda e, uf=uf, o=o, lo=lo, hi=hi, b=banks[s], s=s: e.activation(
                            out=uf[:, o:o + (hi - lo)], in_=pb[b][:, lo - s * SUB:hi - s * SUB], func=AF.Identity),
                            writes=[("pb", banks[s]), ufr])
                    masked_prefix(uf, ufr, HC, 192 + g * HC, l)
                    dt_, dtr = dbf.next()
                    for si, (kind, c0, n) in enumerate(segs):
                        o = offs[si]
                        a = o - HC
                        end = o + n
                        src, srcr = uf, ufr
                        for m in range(g + 1):
                            sh = 2 ** m
                            lo_i = a + 2 ** (m + 1) - 1
                            dst, dstr = ptmp.next()
                            P.op("dve", lambda e, src=src, dst=dst, lo_i=lo_i, end=end, sh=sh: e.tensor_tensor(
                                out=dst[:, lo_i:end], in0=src[:, lo_i:end], in1=src[:, lo_i - sh:end - sh], op=ALU.add),
                                reads=[srcr], writes=[dstr])
                            src, srcr = dst, dstr
                        P.op("dve", lambda e, src=src, o=o, n=n, c0=c0, uf=uf, dt_=dt_, w=w: e.scalar_tensor_tensor(
                            out=dt_[:, c0:c0 + n], in0=src[:, o:o + n], scalar=1.0 / w, in1=uf[:, o:o + n],
                            op0=ALU.mult, op1=ALU.subtract),
                            reads=[srcr, ufr], writes=[dtr])
                        if p == 0 and kind == 0:
                            oo = o + HALO
                            cc = c0 + HALO
                            ib = _off["icnt"] + g * 16
                            tmpt, tmpr = ptmp.next()
                            P.op("dve", lambda e, src=src, oo=oo, ib=ib, tmpt=tmpt: e.tensor_tensor(
                                out=tmpt[:, 0:16], in0=src[:, oo:oo + 16], in1=prm[:, ib:ib + 16], op=ALU.mult),
                                reads=[srcr, "prm"], writes=[tmpr])
                            P.op("dve", lambda e, oo=oo, cc=cc, uf=uf, tmpt=tmpt, dt_=dt_: e.tensor_tensor(
                                out=dt_[:, cc:cc + 16], in0=tmpt[:, 0:16], in1=uf[:, oo:oo + 16], op=ALU.subtract),
                                reads=[tmpr, ufr], writes=[dtr])
                    carry_update(uf, ufr, HC, 192 + g * HC, l)
                    def pool_mm(g=g, dt_=dt_, dtr=dtr):
                        for s in range(2):
                            b = xB.next()[0]
                            cs = slice(s * SUB, (s + 1) * SUB)
                            P.op("pe", lambda e, b=b, cs=cs: e.matmul(pb[b][:, 0:SUB], lhsT=pw_sb[:, l * 4 + g, :], rhs=dt_[:, cs],
                                                                      start=True, stop=True),
                                 reads=["pw", dtr], writes=[("pb", b)])
                            P.op("act", lambda e, b=b, cs=cs: e.activation(out=ycat[:, 12 + g, cs], in_=pb[b][:, 0:SUB], func=AF.Identity,
                                                                           scale=pcol(("psc", l), g)),
                                 reads=["prm"], writes=[("pb", b), ("yc", 12 + g)])
                    pend_pe.append(pool_mm)
            while pend_pe:
                pend_pe.pop(0)()
            bs = (xB.next()[0], xB.next()[0])
            ms = (4, 5)
            for c in range(6):
                sq, sqr = sqb.next()
                P.op("act", lambda e, c=c, sq=sq: e.activation(out=sq[:], in_=convb[:, c, :], func=AF.Square),
                     reads=[("cb", c)], writes=[sqr])
                for s in range(2):
                    cs = slice(s * SUB, (s + 1) * SUB)
                    P.op("pe", lambda e, c=c, s=s, sq=sq, cs=cs: e.matmul(pb[bs[s]][:, 0:SUB], lhsT=ones[:], rhs=sq[:, cs],
                                                                          start=(c == 0), stop=(c == 5)),
                         reads=["ones", sqr], writes=[("pb", bs[s])])
                    P.op("pe", lambda e, c=c, s=s, cs=cs: e.matmul(pb[ms[s]][:, 0:SUB], lhsT=ones[:], rhs=convb[:, c, cs],
                                                                   start=(c == 0), stop=(c == 5)),
                         reads=["ones", ("cb", c)], writes=[("pb", ms[s])])
            ln_steps = []

            def ln_post():
                for s in range(2):
                    cs = slice(s * SUB, (s + 1) * SUB)
                    P.op("act", lambda e, s=s, cs=cs: e.activation(out=lnm[:, cs], in_=pb[ms[s]][:, 0:SUB], func=AF.Identity, scale=1.0 / 768),
                         writes=[("pb", ms[s]), ("lnm", s)])
                    P.op("dve", lambda e, s=s, cs=cs: e.tensor_tensor(out=lnt[:, cs], in0=lnm[:, cs], in1=lnm[:, cs], op=ALU.mult),
                         reads=[("lnm", s)], writes=[("lnt", s)])
                    P.op("dve", lambda e, s=s, cs=cs: e.scalar_tensor_tensor(out=lnt[:, cs], in0=pb[bs[s]][:, 0:SUB], scalar=1.0 / 768,
                                                                            in1=lnt[:, cs], op0=ALU.mult, op1=ALU.subtract),
                         writes=[("pb", bs[s]), ("lnt", s)])
                    P.op("act", lambda e, s=s, cs=cs: e.activation(out=lnr[:, cs], in_=lnt[:, cs], func=AF.Sqrt, scale=1.0, bias=epst[:, 0:1]),
                         reads=["epst", ("lnt", s)], writes=[("lnr", s)])
                    P.op("dve", lambda e, s=s, cs=cs: e.reciprocal(out=lnr[:, cs], in_=lnr[:, cs]), writes=[("lnr", s)])

            def ln_norm(j):
                P.op("dve", lambda e, j=j: e.tensor_tensor(out=convb[:, j, :], in0=convb[:, j, :], in1=lnm[:], op=ALU.subtract),
                     reads=[("lnm", 0), ("lnm", 1)], writes=[("cb", j)])
                P.op("dve", lambda e, j=j: e.tensor_tensor(out=convb[:, j, :], in0=convb[:, j, :], in1=lnr[:], op=ALU.mult),
                     reads=[("lnr", 0), ("lnr", 1)], writes=[("cb", j)])
                P.op("act", lambda e, j=j: e.activation(out=ycat[:, 6 + j, :], in_=convb[:, j, :], func=AF.Silu,
                                                       scale=pcol(("lng", l), j), bias=pcol(("lnb", l), j)),
                     reads=[("cb", j), "prm"], writes=[("yc", 6 + j)])

            ln_steps.append(ln_post)
            for j in range(6):
                ln_steps.append(lambda j=j: ln_norm(j))
            KA = [0, 1, 2, 3, 4, 5, 12, 13, 14, 15]
            KBL = [6, 7, 8, 9, 10, 11]
            tile_i = 0
            for hb in range(8):
                h = hctr[0] % 4
                hctr[0] += 1
                P.dma("pool", lambda e, h=h, hb=hb: e.dma_start(out=halves[h][:, 0:6, :], in_=wov[:, 0:6, hb * 256:(hb + 1) * 256]),
                      ("h", h), writes=[("h", h)])
                P.dma("pool", lambda e, h=h, hb=hb: e.dma_start(out=halves[h][:, 6:10, :], in_=wov[:, 12:16, hb * 256:(hb + 1) * 256]),
                      ("h", h), writes=[("h", h)])
                for t in range(2):
                    d = hb * 2 + t
                    for s in range(2):
                        b = allB.next()[0]
                        cs = slice(s * SUB, (s + 1) * SUB)
                        for i, kc in enumerate(KA):
                            P.op("pe", lambda e, i=i, kc=kc, h=h, t=t, cs=cs, b=b: e.matmul(
                                pb[b][:, 0:SUB], lhsT=halves[h][:, i, t * 128:(t + 1) * 128], rhs=ycat[:, kc, cs],
                                start=(i == 0), stop=(i == len(KA) - 1)),
                                reads=[("h", h), ("yc", kc)], writes=[("pb", b)])
                        P.op("dve", lambda e, d=d, cs=cs, b=b: e.tensor_tensor(out=resid[:, d, cs], in0=pb[b][:, 0:SUB],
                                                                              in1=resid[:, d, cs], op=ALU.add),
                             writes=[("pb", b), ("r", d, s)])
                        tile_i += 1
                        if tile_i % 4 == 2 and ln_steps:
                            ln_steps.pop(0)()
            while ln_steps:
                ln_steps.pop(0)()
            for hb in range(8):
                h = hctr[0] % 4
                hctr[0] += 1
                P.dma("pool", lambda e, h=h, hb=hb: e.dma_start(out=halves[h][:, 0:6, :], in_=wov[:, 6:12, hb * 256:(hb + 1) * 256]),
                      ("h", h), writes=[("h", h)])
                for t in range(2):
                    d = hb * 2 + t
                    for s in range(2):
                        b = dnB.next()[0]
                        cs = slice(s * SUB, (s + 1) * SUB)
                        for i, kc in enumerate(KBL):
                            P.op("pe", lambda e, i=i, kc=kc, h=h, t=t, cs=cs, b=b: e.matmul(
                                pb[b][:, 0:SUB], lhsT=halves[h][:, i, t * 128:(t + 1) * 128], rhs=ycat[:, kc, cs],
                                start=(i == 0), stop=(i == len(KBL) - 1)),
                                reads=[("h", h), ("yc", kc)], writes=[("pb", b)])
                        P.op("dve", lambda e, d=d, cs=cs, b=b: e.tensor_tensor(out=resid[:, d, cs], in0=pb[b][:, 0:SUB],
                                                                              in1=resid[:, d, cs], op=ALU.add),
                             writes=[("pb", b), ("r", d, s)])
                    rms_sq(d, lag=2)

        for p in range(DBG.get("passes", NP)):
            load_x(p)
            for kc in range(KC):
                rms_sq(kc)
            for l in range(DBG.get("layers", L)):
                if DBG.get("ffn1", True):
                    rmsnorm(("n1", l))
                    ffn(l, w_d["ffn1_wg"], w_d["ffn1_wu"], w_d["ffn1_wd"])
                if DBG.get("mixer", True):
                    rmsnorm(("nm", l))
                    mixer(p, l)
                    if DBG.get("dump") and (p, l) == tuple(DBG["dump"]):
                        dbg_d = nc.dram_tensor("dbg", [128, KC, TT], F32, kind="ExternalOutput").ap()
                        P.dma("pool", lambda e: e.dma_start(out=dbg_d, in_=ycat[:]), "dbg", reads=[("yc", k) for k in range(KC)])
                if DBG.get("ffn2", True):
                    rmsnorm(("n2", l))
                    ffn(l, w_d["ffn2_wg"], w_d["ffn2_wu"], w_d["ffn2_wd"])
            final_norm_store(p, p == NP - 1)
        P.dma("sp", lambda e: e.dma_start(out=nb_d, in_=car[:]), "nb", reads=["car"])
        P.emit(nc, final_waits=[("y", "sqb", i) for i in range(3)]
               + [("y", "zev", i) for i in range(6)] + ["nb", "dbg"])
    return nc


def _fm(v, nchunk):
    return np.ascontiguousarray(np.asarray(v, np.float32).reshape(nchunk, 128).T)


def _pack_prm(inp, c):
    prm = np.zeros((128, NPRM), np.float32)

    def put(name, arr):
        arr = np.asarray(arr, np.float32).reshape(128, -1)
        prm[:, _off[name]:_off[name] + arr.shape[1]] = arr

    for l in range(L):
        put(("n1", l), _fm(inp["ffn1_norm"][l], KC))
        put(("nm", l), _fm(inp["mix_norm"][l], KC))
        put(("n2", l), _fm(inp["ffn2_norm"][l], KC))
        put(("caw", l), np.asarray(inp["conv_a_w"][l]).reshape(3, 6, 128).transpose(2, 1, 0))
        put(("cbw", l), np.asarray(inp["conv_b_w"][l]).reshape(31, 6, 128).transpose(2, 1, 0))
        put(("cbb", l), _fm(inp["conv_b_bias"][l], 6))
        put(("lng", l), _fm(inp["ln_b_gain"][l], 6))
        put(("lnb", l), _fm(inp["ln_b_bias"][l], 6))
        put(("psc", l), _fm(inp["pool_scale"][l], 4))
    put("nf", _fm(inp["final_norm"], KC))
    q = c % 4
    prm[:, _off["hmask"]] = 0.0 if q == 0 else 1.0
    ic = np.zeros((4, 16), np.float32)
    for g in range(4):
        w = 2 ** (g + 1)
        for i in range(16):
            ic[g, i] = (1.0 / min(i + 1, w)) if q == 0 else (1.0 / w)
    prm[:, _off["icnt"]:_off["icnt"] + 64] = ic.reshape(1, 64)
    cache = np.zeros((128, L, CARW), np.float32)
    for l in range(L):
        cache[:, l, 0:12] = np.asarray(inp["cache_conv_a"][l, c]).reshape(HA, 6, 128).transpose(2, 1, 0).reshape(128, 12)
        cache[:, l, 12:192] = np.asarray(inp["cache_conv_b"][l, c]).reshape(HB, 6, 128).transpose(2, 1, 0).reshape(128, 180)
        cache[:, l, 192:252] = np.asarray(inp["cache_pool"][l, c]).reshape(HC, 4, 128).transpose(2, 1, 0).reshape(128, 60)
    prm[:, _off["cache"]:_off["cache"] + L * CARW] = cache.reshape(128, L * CARW)
    return prm


def kernel(**inputs):
    inp = {k: np.asarray(v) for k, v in inputs.items()}
    xp = inp["x_prompt"].astype(np.float32, copy=False)
    xs = inp["x_sample"].astype(np.float32, copy=False)
    B, S, _ = xp.shape
    QL = S // 4
    wnames = ("ffn1_wg", "ffn1_wu", "ffn1_wd", "w_in", "pool_w", "w_out", "ffn2_wg", "ffn2_wu", "ffn2_wd")
    wts = {k: np.ascontiguousarray(inp[k], dtype=np.float32) for k in wnames}
    in_maps = []
    for c in range(NCORES):
        b, q = c // 4, c % 4
        halo = np.zeros((HALO, D), np.float32) if q == 0 else xp[b, q * QL - HALO:q * QL]
        toks = np.concatenate([xs[c], halo, xp[b, q * QL:(q + 1) * QL]], axis=0)
        xT = np.ascontiguousarray(toks.reshape(NP, TT, KC, 128).transpose(3, 0, 2, 1))
        m = {"xT": xT, "prm": _pack_prm(inp, c)}
        m.update(wts)
        in_maps.append(m)
    nc = build_nc()
    res = run_bass_kernel_spmd(nc, in_maps, core_ids=list(range(NCORES)))
    y_prompt = np.zeros((B, S, D), np.float32)
    y_sample = np.zeros(xs.shape, np.float32)
    na_p = np.zeros((L, B, HA, 768), np.float32)
    nb_p = np.zeros((L, B, HB, 768), np.float32)
    np_p = np.zeros((L, B, HC, 512), np.float32)
    na_s = np.zeros((L, NCORES, HA, 768), np.float32)
    nb_s = np.zeros((L, NCORES, HB, 768), np.float32)
    np_s = np.zeros((L, NCORES, HC, 512), np.float32)
    for c in range(NCORES):
        b, q = c // 4, c % 4
        r = res.results[c]
        toks = np.asarray(r["yT"]).transpose(1, 3, 2, 0).reshape(NP * TT, D)
        y_sample[c] = toks[0:SAMP]
        y_prompt[b, q * QL:(q + 1) * QL] = toks[SAMP + HALO:]
        nbuf = np.asarray(r["nbuf"]).reshape(128, 2, L, CARW)
        for l in range(L):
            for kind, (da, db, dp, idx) in ((1, (na_s, nb_s, np_s, c)), (0, (na_p, nb_p, np_p, b))):
                if kind == 0 and q != 3:
                    continue
                v = nbuf[:, kind, l]
                da[l, idx] = v[:, 0:12].reshape(128, 6, HA).transpose(2, 1, 0).reshape(HA, 768)
                db[l, idx] = v[:, 12:192].reshape(128, 6, HB).transpose(2, 1, 0).reshape(HB, 768)
                dp[l, idx] = v[:, 192:252].reshape(128, 4, HC).transpose(2, 1, 0).reshape(HC, 512)
    return (y_prompt, y_sample, na_p, nb_p, np_p, na_s, nb_s, np_s)
```

```python
import numpy as np
from contextlib import ExitStack
import concourse.bass as bass
import concourse.mybir as mybir
from concourse.bass_utils import run_bass_kernel_spmd

F32 = mybir.dt.float32
BF16 = mybir.dt.bfloat16
ALU = mybir.AluOpType
AF = mybir.ActivationFunctionType

D = 2048
KC = 16
FF = 5632
NF = 44
GF = 2
NG = NF // GF
DIN = 4352
L = 2
TT = 544
SUB = 272
NP = 4
HALO = 64
SAMP = 64
HA, HB, HC = 2, 30, 15
CARW = 6 * HA + 6 * HB + 4 * HC
EPS = 1e-6
NCORES = 8
DBG = {}
SELF_SYNC_WINDOW = 6

_off = {}
_n = 0


def _alloc(name, n):
    global _n
    _off[name] = _n
    _n += n


for _l in range(L):
    _alloc(("n1", _l), KC)
    _alloc(("nm", _l), KC)
    _alloc(("n2", _l), KC)
    _alloc(("caw", _l), 6 * 3)
    _alloc(("cbw", _l), 6 * 31)
    _alloc(("cbb", _l), 6)
    _alloc(("lng", _l), 6)
    _alloc(("lnb", _l), 6)
    _alloc(("psc", _l), 4)
_alloc("nf", KC)
_alloc("hmask", 1)
_alloc("icnt", 4 * 16)
_alloc("cache", L * CARW)
NPRM = _n

ENGS = ("pe", "act", "dve", "pool", "sp")


class Op:
    __slots__ = ("eng", "fn", "reads", "writes", "dma", "key", "idx", "deps", "milestone", "mval")

    def __init__(self, eng, fn, reads, writes, dma, key):
        self.eng = eng
        self.fn = fn
        self.reads = reads
        self.writes = writes
        self.dma = dma
        self.key = key
        self.deps = None
        self.milestone = False
        self.mval = 0


class Prog:
    def __init__(self):
        self.ops = []

    def op(self, eng, fn, reads=(), writes=()):
        self.ops.append(Op(eng, fn, tuple(reads), tuple(writes), False, None))

    def dma(self, eng, fn, key, reads=(), writes=()):
        self.ops.append(Op(eng, fn, tuple(reads), tuple(writes), True, key))

    def analyze(self):
        last_w = {}
        readers = {}
        ops = self.ops
        epos = {}
        pos = [0] * len(ops)
        for i, o in enumerate(ops):
            pos[i] = epos.get(o.eng, 0)
            epos[o.eng] = pos[i] + 1
        for i, o in enumerate(ops):
            o.idx = i
            deps = set()
            for r in o.reads:
                w = last_w.get(r)
                if w is not None:
                    deps.add(w)
            for w_ in o.writes:
                w = last_w.get(w_)
                if w is not None:
                    deps.add(w)
                rs = readers.get(w_)
                if rs:
                    deps.update(rs)
            deps.discard(i)
            keep = set()
            for d in deps:
                p = ops[d]
                if p.dma:
                    if o.dma and o.key == p.key:
                        continue
                    keep.add(d)
                elif p.eng == o.eng and not o.dma:
                    if o.eng != "pe" and pos[i] - pos[d] <= SELF_SYNC_WINDOW:
                        keep.add(d)
                    continue
                else:
                    keep.add(d)
            o.deps = keep
            for w_ in o.writes:
                last_w[w_] = i
                readers[w_] = []
            for r in o.reads:
                readers.setdefault(r, []).append(i)
        for o in ops:
            for d in o.deps:
                ops[d].milestone = True
        cnt = {}
        for o in ops:
            if o.dma:
                k = ("dma", o.key)
                cnt[k] = cnt.get(k, 0) + 16
                o.mval = cnt[k]
            elif o.milestone:
                k = ("eng", o.eng)
                cnt[k] = cnt.get(k, 0) + 1
                o.mval = cnt[k]
        self.sem_final = cnt

    def emit(self, nc, final_waits=()):
        self.analyze()
        ops = self.ops
        with ExitStack() as es:
            sems = {}
            for k in self.sem_final:
                nm = "s_" + "_".join(str(x) for x in (k[1] if isinstance(k[1], tuple) else (k[1],)))
                sems[k] = es.enter_context(nc.semaphore(nm))
            block = es.enter_context(nc.Block())
            per_eng = {e: [] for e in ENGS}
            for o in ops:
                per_eng[o.eng].append(o)

            def run(engine_obj, ename):
                waited = {}
                for o in per_eng[ename]:
                    need = {}
                    for d in o.deps:
                        p = ops[d]
                        k = ("dma", p.key) if p.dma else ("eng", p.eng)
                        if p.mval > need.get(k, 0):
                            need[k] = p.mval
                    for k, v in need.items():
                        if waited.get(k, 0) >= v:
                            continue
                        engine_obj.wait_ge(sems[k], v)
                        waited[k] = v
                    ins = o.fn(engine_obj)
                    if o.dma:
                        ins.then_inc(sems[("dma", o.key)], 16)
                    elif o.milestone:
                        ins.then_inc(sems[("eng", o.eng)], 1)
                if ename == "sp":
                    for key in final_waits:
                        k = ("dma", key)
                        if k in sems:
                            engine_obj.wait_ge(sems[k], self.sem_final[k])

            @block.tensor
            def _(e):
                run(e, "pe")

            @block.scalar
            def _(e):
                run(e, "act")

            @block.vector
            def _(e):
                run(e, "dve")

            @block.gpsimd
            def _(e):
                run(e, "pool")

            @block.sync
            def _(e):
                run(e, "sp")


class Rot:
    def __init__(self, tiles, name):
        self.tiles = tiles
        self.name = name
        self.i = 0

    def next(self):
        k = self.i % len(self.tiles)
        self.i += 1
        return self.tiles[k], (self.name, k)


def segs_of(p):
    if p == 0:
        return [(1, 0, SAMP), (0, SAMP, TT - SAMP)]
    return [(0, 0, TT)]


def seg_off(p, H):
    offs = []
    o = 0
    for (_, _, n) in segs_of(p):
        offs.append(o + H)
        o += H + n
    return offs


def pieces(p):
    out = []
    for s in range(2):
        a, b = s * SUB, (s + 1) * SUB
        for si, (_, c0, n) in enumerate(segs_of(p)):
            lo, hi = max(a, c0), min(b, c0 + n)
            if lo < hi:
                out.append((s, lo, hi, si))
    return out


def build_nc():
    nc = bass.Bass("TRN2", target_bir_lowering=False)
    dt = lambda name, shape, kind: nc.dram_tensor(name, shape, F32, kind=kind).ap()
    xT = dt("xT", [128, NP, KC, TT], "ExternalInput")
    prm_d = dt("prm", [128, NPRM], "ExternalInput")
    w_d = {}
    for nm, shp in (("ffn1_wg", [L, D, FF]), ("ffn1_wu", [L, D, FF]), ("ffn1_wd", [L, FF, D]),
                    ("w_in", [L, D, DIN]), ("pool_w", [L, 4, 128, 128]), ("w_out", [L, D, D]),
                    ("ffn2_wg", [L, D, FF]), ("ffn2_wu", [L, D, FF]), ("ffn2_wd", [L, FF, D])):
        w_d[nm] = dt(nm, shp, "ExternalInput")
    yT = dt("yT", [128, NP, KC, TT], "ExternalOutput")
    nb_d = dt("nbuf", [128, 2 * L * CARW], "ExternalOutput")

    P = Prog()
    with ExitStack() as es:
        sb = lambda n, s, d=F32: es.enter_context(nc.sbuf_tensor(n, s, d))
        resid = sb("resid", [128, KC, TT])
        xn = sb("xn", [128, KC, TT], BF16)
        ycat = sb("ycat", [128, KC, TT], BF16)
        convb = sb("convb", [128, 6, TT])
        hid = [sb(f"hid{i}", [128, GF, TT], BF16) for i in range(2)]
        halves = [sb(f"wh{i}", [128, KC, 256], BF16) for i in range(4)]
        wds = [sb(f"wd{i}", [128, GF, D], BF16) for i in range(2)]
        PADW = TT + 2 * HB
        sqb = Rot([sb(f"sqb{i}", [128, TT]) for i in range(3)], "sqb")
        sgb = Rot([sb(f"sg{i}", [128, SUB]) for i in range(4)], "sg")
        zev = Rot([sb(f"zev{i}", [128, TT]) for i in range(6)], "zev")
        padb = Rot([sb(f"pad{i}", [128, PADW]) for i in range(3)], "pad")
        ptmp = Rot([sb(f"ptmp{i}", [128, PADW]) for i in range(2)], "ptmp")
        dbf = Rot([sb(f"dbf{i}", [128, TT], BF16) for i in range(2)], "dbf")
        rstd = sb("rstd", [128, TT])
        sqacc = [sb(f"sqacc{i}", [128, TT]) for i in range(4)]
        lnm = sb("lnm", [128, TT])
        lnr = sb("lnr", [128, TT])
        lnt = sb("lnt", [128, TT])
        prm = sb("prm_sb", [128, NPRM])
        car = sb("car", [128, 2 * L * CARW])
        pw_sb = sb("pw_sb", [128, L * 4, 128], BF16)
        ones = sb("ones", [128, 128])
        epst = sb("epst", [128, 1])
        dmy = sb("dmy", [128, 4])
        pb = [es.enter_context(nc.psum_tensor(f"pb{i}", [128, 512], F32)) for i in range(8)]
        guB = Rot(list(range(0, 4)), "guB")
        dnB = Rot(list(range(4, 8)), "dnB")
        zB = Rot(list(range(0, 6)), "zB")
        allB = Rot(list(range(0, 8)), "allB")
        xB = Rot(list(range(6, 8)), "xB")
        hctr = [0]
        wdctr = [0]

        def pcol(name, i=0):
            o = _off[name] + i
            return prm[:, o:o + 1]

        def car_sl(kind, l, o, n):
            base = (kind * L + l) * CARW + o
            return car[:, base:base + n]

        P.dma("sp", lambda e: e.dma_start(out=prm[:], in_=prm_d), "prm", writes=["prm"])
        P.dma("pool", lambda e: e.dma_start(out=pw_sb[:], in_=w_d["pool_w"].rearrange("l g c d -> c (l g) d")),
              "prm2", writes=["pw"])
        P.op("dve", lambda e: e.memset(ones[:], 1.0), writes=["ones"])
        P.op("dve", lambda e: e.memset(epst[:], EPS), writes=["epst"])
        P.op("dve", lambda e: e.memset(car[:, 0:L * CARW], 0.0), writes=["car"])
        co = _off["cache"]
        P.op("dve", lambda e: e.tensor_copy(out=car[:, L * CARW:2 * L * CARW], in_=prm[:, co:co + L * CARW]),
             reads=["prm"], writes=["car"])

        def load_x(p):
            for q in range(8):
                P.dma("sp", lambda e, q=q: e.dma_start(out=resid[:, 2 * q:2 * q + 2, :], in_=xT[:, p, 2 * q:2 * q + 2, :]),
                      ("x", q), writes=[("r", kc, s) for kc in range(2 * q, 2 * q + 2) for s in range(2)])

        def sumsq_rstd(src_fn, nchunks, src_res_fn, out_tile, out_res, scale, mean_tile=None):
            b0, b1 = xB.next()[0], xB.next()[0]
            bs = (b0, b1)
            if mean_tile is not None:
                ms = (4, 5)
            for c in range(nchunks):
                sq, sqr = sqb.next()
                P.op("act", lambda e, c=c, sq=sq: e.activation(out=sq[:], in_=src_fn(c), func=AF.Square),
                     reads=src_res_fn(c), writes=[sqr])
                for s in range(2):
                    P.op("pe", lambda e, c=c, s=s, sq=sq: e.matmul(pb[bs[s]][:, 0:SUB], lhsT=ones[:], rhs=sq[:, s * SUB:(s + 1) * SUB],
                                                                  start=(c == 0), stop=(c == nchunks - 1)),
                         reads=["ones", sqr], writes=[("pb", bs[s])])
                    if mean_tile is not None:
                        P.op("pe", lambda e, c=c, s=s: e.matmul(pb[ms[s]][:, 0:SUB], lhsT=ones[:], rhs=src_fn(c)[:, s * SUB:(s + 1) * SUB],
                                                               start=(c == 0), stop=(c == nchunks - 1)),
                             reads=["ones"] + src_res_fn(c), writes=[("pb", ms[s])])
            for s in range(2):
                cs = slice(s * SUB, (s + 1) * SUB)
                if mean_tile is None:
                    P.op("act", lambda e, s=s, cs=cs: e.activation(out=out_tile[:, cs], in_=pb[bs[s]][:, 0:SUB], func=AF.Sqrt,
                                                                  scale=scale, bias=epst[:, 0:1]),
                         reads=["epst"], writes=[("pb", bs[s]), (out_res, s)])
                else:
                    P.op("act", lambda e, s=s, cs=cs: e.activation(out=mean_tile[:, cs], in_=pb[ms[s]][:, 0:SUB], func=AF.Identity,
                                                                  scale=scale),
                         writes=[("pb", ms[s]), ("lnm", s)])
                    P.op("dve", lambda e, s=s, cs=cs: e.tensor_tensor(out=lnt[:, cs], in0=mean_tile[:, cs], in1=mean_tile[:, cs], op=ALU.mult),
                         reads=[("lnm", s)], writes=[("lnt", s)])
                    P.op("dve", lambda e, s=s, cs=cs: e.scalar_tensor_tensor(out=lnt[:, cs], in0=pb[bs[s]][:, 0:SUB], scalar=scale,
                                                                            in1=lnt[:, cs], op0=ALU.mult, op1=ALU.subtract),
                         writes=[("pb", bs[s]), ("lnt", s)])
                    P.op("act", lambda e, s=s, cs=cs: e.activation(out=out_tile[:, cs], in_=lnt[:, cs], func=AF.Sqrt,
                                                                  scale=1.0, bias=epst[:, 0:1]),
                         reads=["epst", ("lnt", s)], writes=[(out_res, s)])
                P.op("dve", lambda e, s=s, cs=cs: e.reciprocal(out=out_tile[:, cs], in_=out_tile[:, cs]),
                     writes=[(out_res, s)])

        _pl = {AF.Sqrt: 0, AF.Silu: 1, AF.Sigmoid: 2}

        def act_preload(func):
            i = _pl[func]
            P.op("act", lambda e: e.activation(out=dmy[:, i:i + 1], in_=epst[:, 0:1], func=func),
                 reads=["epst"], writes=[("dmy", i)])

        pending_adds = []

        def rms_sq(kc, lag=0):
            q, r = kc // 4, kc % 4
            if r == 0:
                P.op("act", lambda e: e.activation(out=sqacc[q][:], in_=resid[:, kc, :], func=AF.Square),
                     reads=[("r", kc, 0), ("r", kc, 1)], writes=[("sqacc", q)])
            else:
                sq, sqr = sqb.next()
                P.op("act", lambda e: e.activation(out=sq[:], in_=resid[:, kc, :], func=AF.Square),
                     reads=[("r", kc, 0), ("r", kc, 1)], writes=[sqr])
                pending_adds.append((q, sq, sqr))
            while len(pending_adds) > lag:
                q2, sq2, sqr2 = pending_adds.pop(0)
                P.op("dve", lambda e, q2=q2, sq2=sq2: e.tensor_tensor(out=sqacc[q2][:], in0=sqacc[q2][:], in1=sq2[:], op=ALU.add),
                     reads=[sqr2], writes=[("sqacc", q2)])

        def rms_flush():
            while pending_adds:
                q2, sq2, sqr2 = pending_adds.pop(0)
                P.op("dve", lambda e, q2=q2, sq2=sq2: e.tensor_tensor(out=sqacc[q2][:], in0=sqacc[q2][:], in1=sq2[:], op=ALU.add),
                     reads=[sqr2], writes=[("sqacc", q2)])

        def rms_finish(next_func=None):
            rms_flush()
            bs = (xB.next()[0], xB.next()[0])
            for q in range(4):
                for s in range(2):
                    P.op("pe", lambda e, q=q, s=s: e.matmul(pb[bs[s]][:, 0:SUB], lhsT=ones[:], rhs=sqacc[q][:, s * SUB:(s + 1) * SUB],
                                                           start=(q == 0), stop=(q == 3)),
                         reads=["ones", ("sqacc", q)], writes=[("pb", bs[s])])
            for s in range(2):
                cs = slice(s * SUB, (s + 1) * SUB)
                P.op("act", lambda e, s=s, cs=cs: e.activation(out=rstd[:, cs], in_=pb[bs[s]][:, 0:SUB], func=AF.Sqrt,
                                                              scale=1.0 / D, bias=epst[:, 0:1]),
                     reads=["epst"], writes=[("pb", bs[s]), ("rstd", s)])
                P.op("dve", lambda e, s=s, cs=cs: e.reciprocal(out=rstd[:, cs], in_=rstd[:, cs]),
                     writes=[("rstd", s)])
            if next_func is not None:
                act_preload(next_func)

        def rmsnorm(gname, next_func=None):
            rms_finish(next_func)
            for kc in range(KC):
                P.op("dve", lambda e, kc=kc: e.scalar_tensor_tensor(out=xn[:, kc, :], in0=resid[:, kc, :], scalar=pcol(gname, kc),
                                                                   in1=rstd[:], op0=ALU.mult, op1=ALU.mult),
                     reads=[("r", kc, 0), ("r", kc, 1), ("rstd", 0), ("rstd", 1), "prm"], writes=[("xn", kc)])

        def final_norm_store(p, last):
            rms_finish()
            stage = ([(sqb.tiles[i], ("sqb", i)) for i in range(3)] + [(zev.tiles[i], ("zev", i)) for i in range(6)])
            for kc in range(KC):
                yt, ytr = stage[kc % len(stage)]
                P.op("dve", lambda e, kc=kc, yt=yt: e.scalar_tensor_tensor(out=yt[:], in0=resid[:, kc, :], scalar=pcol("nf", kc),
                                                                          in1=rstd[:], op0=ALU.mult, op1=ALU.mult),
                     reads=[("r", kc, 0), ("r", kc, 1), ("rstd", 0), ("rstd", 1), "prm"], writes=[ytr])
                P.dma("sp", lambda e, kc=kc, yt=yt: e.dma_start(out=yT[:, p, kc, :], in_=yt[:]), ("y",) + ytr, reads=[ytr])

        def ffn(l, wg, wu, wd):
            wgv = wg[l].rearrange("(kc p) f -> p kc f", p=128)
            wuv = wu[l].rearrange("(kc p) f -> p kc f", p=128)
            wdv = wd[l].rearrange("(fc p) d -> p fc d", p=128)
            slots = {}

            def dn_tile(gg, d, s, rot=None):
                hg, hu, wdi = slots[gg]
                b = (rot or dnB).next()[0]
                cs = slice(s * SUB, (s + 1) * SUB)
                for fi in range(GF):
                    P.op("pe", lambda e, fi=fi: e.matmul(pb[b][:, 0:SUB], lhsT=wds[wdi][:, fi, d * 128:(d + 1) * 128],
                                                        rhs=hid[gg % 2][:, fi, cs], start=(fi == 0), stop=(fi == GF - 1)),
                         reads=[("wd", wdi), ("hd", gg % 2, fi, s)], writes=[("pb", b)])
                P.op("dve", lambda e: e.scalar_tensor_tensor(out=resid[:, d, cs], in0=pb[b][:, 0:SUB], scalar=0.5,
                                                            in1=resid[:, d, cs], op0=ALU.mult, op1=ALU.add),
                     writes=[("pb", b), ("r", d, s)])

            for g in range(NG + 1):
                gu_tiles = []
                if g < NG:
                    hg = hctr[0] % 4
                    hu = (hctr[0] + 1) % 4
                    hctr[0] += 2
                    wdi = wdctr[0] % 2
                    wdctr[0] += 1
                    slots[g] = (hg, hu, wdi)
                    f0 = g * GF * 128
                    P.dma("pool", lambda e, hg=hg, f0=f0: e.dma_start(out=halves[hg][:], in_=wgv[:, :, f0:f0 + 256]),
                          ("h", hg), writes=[("h", hg)])
                    P.dma("pool", lambda e, hu=hu, f0=f0: e.dma_start(out=halves[hu][:], in_=wuv[:, :, f0:f0 + 256]),
                          ("h", hu), writes=[("h", hu)])
                    P.dma("pool", lambda e, wdi=wdi, g=g: e.dma_start(out=wds[wdi][:], in_=wdv[:, g * GF:(g + 1) * GF, :]),
                          ("wd", wdi), writes=[("wd", wdi)])
                    for fi in range(GF):
                        for gu in range(2):
                            for s in range(2):
                                gu_tiles.append((fi, gu, s))
                dn_tiles = [(d, s) for d in range(KC) for s in range(2)] if g >= 1 else []
                sgmap = {}
                nsteps = max(len(gu_tiles), 8)
                if g == NG:
                    act_preload(AF.Sqrt)
                pre_banks = {}
                if g == 0:
                    for i in range(4):
                        pre_banks[i] = guB.next()[0]
                    for kc in range(KC):
                        for i in range(4):
                            fi, gu, s = gu_tiles[i]
                            hsel = hg if gu == 0 else hu
                            b = pre_banks[i]
                            cs = slice(s * SUB, (s + 1) * SUB)
                            P.op("pe", lambda e, kc=kc, hsel=hsel, fi=fi, cs=cs, b=b: e.matmul(
                                pb[b][:, 0:SUB], lhsT=halves[hsel][:, kc, fi * 128:(fi + 1) * 128], rhs=xn[:, kc, cs],
                                start=(kc == 0), stop=(kc == KC - 1)),
                                reads=[("h", hsel), ("xn", kc)], writes=[("pb", b)])
                for i in range(nsteps):
                    if i < len(gu_tiles):
                        fi, gu, s = gu_tiles[i]
                        hsel = hg if gu == 0 else hu
                        cs = slice(s * SUB, (s + 1) * SUB)
                        if i in pre_banks:
                            b = pre_banks[i]
                        else:
                            b = guB.next()[0]
                            for kc in range(KC):
                                P.op("pe", lambda e, kc=kc, hsel=hsel, fi=fi, cs=cs, b=b: e.matmul(
                                    pb[b][:, 0:SUB], lhsT=halves[hsel][:, kc, fi * 128:(fi + 1) * 128], rhs=xn[:, kc, cs],
                                    start=(kc == 0), stop=(kc == KC - 1)),
                                    reads=[("h", hsel), ("xn", kc)], writes=[("pb", b)])
                        if gu == 0:
                            sg, sgr = sgb.next()
                            sgmap[(fi, s)] = (sg, sgr)
                            P.op("act", lambda e, sg=sg, b=b: e.activation(out=sg[:], in_=pb[b][:, 0:SUB], func=AF.Silu),
                                 writes=[("pb", b), sgr])
                        else:
                            sg, sgr = sgmap[(fi, s)]
                            P.op("dve", lambda e, sg=sg, b=b, fi=fi, cs=cs, g=g: e.tensor_tensor(
                                out=hid[g % 2][:, fi, cs], in0=pb[b][:, 0:SUB], in1=sg[:], op=ALU.mult),
                                reads=[sgr], writes=[("pb", b), ("hd", g % 2, fi, s)])
                    if dn_tiles:
                        for (d, s) in dn_tiles[4 * i:4 * i + 4]:
                            dn_tile(g - 1, d, s, allB if g == NG else None)
                            if g == NG and s == 1:
                                rms_sq(d, lag=2)

        def mixer(p, l):
            segs = segs_of(p)
            pcs = pieces(p)
            winv = w_d["w_in"][l].rearrange("(kc p) f -> p kc f", p=128)
            wov = w_d["w_out"][l].rearrange("(kc p) f -> p kc f", p=128)
            Z = []
            for j in range(6):
                Z.append(("gg", j, 3072 + j * 128))
                Z.append(("ga", j, 2304 + j * 128))
                Z.append(("ha", j, 0 + j * 128))
                Z.append(("ca", j, 1536 + j * 128))
                Z.append(("ba", j, 768 + j * 128))
            Z = [("up", g, 3840 + g * 128) for g in range(4)] + Z
            held = {}
            pend_pe = []

            def masked_prefix(buf, bufr, H, caro, l):
                offs = seg_off(p, H)
                if p == 0:
                    o1 = offs[1]
                    P.op("dve", lambda e: e.tensor_scalar(out=buf[:, o1:o1 + HALO], in0=buf[:, o1:o1 + HALO],
                                                         scalar1=pcol("hmask"), scalar2=None, op0=ALU.mult),
                         reads=["prm"], writes=[bufr])
                for si, (kind, c0, n) in enumerate(segs):
                    o = offs[si]
                    P.op("dve", lambda e, o=o, kind=kind: e.tensor_copy(out=buf[:, o - H:o], in_=car_sl(kind, l, caro, H)),
                         reads=["car"], writes=[bufr])

            def carry_update(buf, bufr, H, caro, l):
                offs = seg_off(p, H)
                for si, (kind, c0, n) in enumerate(segs):
                    o = offs[si]
                    P.op("dve", lambda e, o=o, n=n, kind=kind: e.tensor_copy(out=car_sl(kind, l, caro, H), in_=buf[:, o + n - H:o + n]),
                         reads=[bufr], writes=["car"])

            for zi, (typ, j, col) in enumerate(Z):
                if zi % 2 == 0:
                    h = hctr[0] % 4
                    hctr[0] += 1
                    cur_h = h
                    for t in range(2):
                        if zi + t < len(Z):
                            c2 = Z[zi + t][2]
                            P.dma("pool", lambda e, h=h, t=t, c2=c2: e.dma_start(out=halves[h][:, :, t * 128:(t + 1) * 128],
                                                                                 in_=winv[:, :, c2:c2 + 128]),
                                  ("h", h), writes=[("h", h)])
                hh = cur_h
                t = zi % 2
                if zi == 0:
                    pre_banks = {0: [zB.next()[0], zB.next()[0]], 1: [zB.next()[0], zB.next()[0]]}
                    for kc in range(KC):
                        for t2 in range(2):
                            for s in range(2):
                                b = pre_banks[t2][s]
                                cs = slice(s * SUB, (s + 1) * SUB)
                                P.op("pe", lambda e, kc=kc, hh=hh, t2=t2, cs=cs, b=b: e.matmul(
                                    pb[b][:, 0:SUB], lhsT=halves[hh][:, kc, t2 * 128:(t2 + 1) * 128], rhs=xn[:, kc, cs],
                                    start=(kc == 0), stop=(kc == KC - 1)),
                                    reads=[("h", hh), ("xn", kc)], writes=[("pb", b)])
                if zi < 2:
                    banks = pre_banks[zi]
                else:
                    banks = []
                    for s in range(2):
                        b = zB.next()[0]
                        banks.append(b)
                        cs = slice(s * SUB, (s + 1) * SUB)
                        for kc in range(KC):
                            P.op("pe", lambda e, kc=kc, hh=hh, t=t, cs=cs, b=b: e.matmul(
                                pb[b][:, 0:SUB], lhsT=halves[hh][:, kc, t * 128:(t + 1) * 128], rhs=xn[:, kc, cs],
                                start=(kc == 0), stop=(kc == KC - 1)),
                                reads=[("h", hh), ("xn", kc)], writes=[("pb", b)])
                while pend_pe:
                    pend_pe.pop(0)()
                if typ in ("gg", "ga", "ha", "ca", "ba"):
                    zt, ztr = zev.next()
                    fn = AF.Sigmoid if typ == "gg" else AF.Identity
                    for s in range(2):
                        cs = slice(s * SUB, (s + 1) * SUB)
                        P.op("act", lambda e, zt=zt, cs=cs, b=banks[s], fn=fn: e.activation(out=zt[:, cs], in_=pb[b][:, 0:SUB], func=fn),
                             writes=[("pb", banks[s]), ztr])
                    held[typ] = (zt, ztr)
                if typ == "ga":
                    sg_t, sg_r = held["gg"]
                    ga_t, ga_r = held["ga"]
                    vf, vfr = padb.next()
                    offs = seg_off(p, HB)
                    for si, (kind, c0, n) in enumerate(segs):
                        o = offs[si]
                        P.op("dve", lambda e, o=o, c0=c0, n=n, vf=vf, ga_t=ga_t, sg_t=sg_t: e.tensor_tensor(
                            out=vf[:, o:o + n], in0=ga_t[:, c0:c0 + n], in1=sg_t[:, c0:c0 + n], op=ALU.mult),
                            reads=[sg_r, ga_r], writes=[vfr])
                    masked_prefix(vf, vfr, HB, 12 + j * HB, l)
                    carry_update(vf, vfr, HB, 12 + j * HB, l)
                    wb = _off[("cbw", l)] + j * 31
                    NACC = 4
                    for si, (kind, c0, n) in enumerate(segs):
                        o = offs[si] - HB
                        accs = [(lambda c0=c0, n=n, j=j: convb[:, j, c0:c0 + n], [("cb", j)]),
                                (lambda c0=c0, n=n: lnm[:, c0:c0 + n], [("lnm", 0), ("lnm", 1)]),
                                (lambda c0=c0, n=n: lnt[:, c0:c0 + n], [("lnt", 0), ("lnt", 1)]),
                                (lambda c0=c0, n=n: lnr[:, c0:c0 + n], [("lnr", 0), ("lnr", 1)])]
                        for k in range(31):
                            afn, ares = accs[k % NACC]
                            if k == 0:
                                P.op("act", lambda e, o=o, n=n, vf=vf, wb=wb, afn=afn, j=j: e.activation(
                                    out=afn(), in_=vf[:, o:o + n], func=AF.Identity, scale=prm[:, wb:wb + 1],
                                    bias=pcol(("cbb", l), j)),
                                    reads=[vfr, "prm"], writes=ares)
                            elif k < NACC:
                                P.op("act", lambda e, o=o, n=n, vf=vf, wb=wb, afn=afn, k=k: e.activation(
                                    out=afn(), in_=vf[:, o + k:o + k + n], func=AF.Identity, scale=prm[:, wb + k:wb + k + 1]),
                                    reads=[vfr, "prm"], writes=ares)
                            else:
                                P.op("dve", lambda e, o=o, n=n, vf=vf, wb=wb, afn=afn, k=k: e.scalar_tensor_tensor(
                                    out=afn(), in0=vf[:, o + k:o + k + n], scalar=prm[:, wb + k:wb + k + 1],
                                    in1=afn(), op0=ALU.mult, op1=ALU.add),
                                    reads=[vfr, "prm"], writes=ares)
                        P.op("dve", lambda e, a0=accs[0][0], a1=accs[1][0]: e.tensor_tensor(out=a0(), in0=a0(), in1=a1(), op=ALU.add),
                             reads=accs[1][1], writes=accs[0][1])
                        P.op("dve", lambda e, a2=accs[2][0], a3=accs[3][0]: e.tensor_tensor(out=a2(), in0=a2(), in1=a3(), op=ALU.add),
                             reads=accs[3][1], writes=accs[2][1])
                        P.op("dve", lambda e, a0=accs[0][0], a2=accs[2][0]: e.tensor_tensor(out=a0(), in0=a0(), in1=a2(), op=ALU.add),
                             reads=accs[2][1], writes=accs[0][1])
                elif typ == "ca":
                    ha_t, ha_r = held["ha"]
                    ca_t, ca_r = held["ca"]
                    cf, cfr = padb.next()
                    offs = seg_off(p, HA)
                    for si, (kind, c0, n) in enumerate(segs):
                        o = offs[si]
                        P.op("dve", lambda e, o=o, c0=c0, n=n, cf=cf, ca_t=ca_t, ha_t=ha_t: e.tensor_tensor(
                            out=cf[:, o:o + n], in0=ca_t[:, c0:c0 + n], in1=ha_t[:, c0:c0 + n], op=ALU.mult),
                            reads=[ha_r, ca_r], writes=[cfr])
                    masked_prefix(cf, cfr, HA, j * HA, l)
                    cva, cvar = zev.next()
                    wb = _off[("caw", l)] + j * 3
                    for si, (kind, c0, n) in enumerate(segs):
                        o = offs[si] - HA
                        P.op("dve", lambda e, o=o, c0=c0, n=n, cf=cf, cva=cva, wb=wb: e.tensor_scalar(
                            out=cva[:, c0:c0 + n], in0=cf[:, o:o + n], scalar1=prm[:, wb:wb + 1], scalar2=None, op0=ALU.mult),
                            reads=[cfr, "prm"], writes=[cvar])
                        for k in range(1, 3):
                            P.op("dve", lambda e, o=o, c0=c0, n=n, cf=cf, cva=cva, wb=wb, k=k: e.scalar_tensor_tensor(
                                out=cva[:, c0:c0 + n], in0=cf[:, o + k:o + k + n], scalar=prm[:, wb + k:wb + k + 1],
                                in1=cva[:, c0:c0 + n], op0=ALU.mult, op1=ALU.add),
                                reads=[cfr, "prm"], writes=[cvar])
                    carry_update(cf, cfr, HA, j * HA, l)
                    held["cva"] = (cva, cvar)
                elif typ == "ba":
                    ba_t, ba_r = held["ba"]
                    cva, cvar = held["cva"]
                    P.op("dve", lambda e, j=j, ba_t=ba_t, cva=cva: e.tensor_tensor(out=ycat[:, j, :], in0=ba_t[:], in1=cva[:], op=ALU.mult),
                         reads=[ba_r, cvar], writes=[("yc", j)])
                elif typ == "up":
                    g = j
                    w = 2 ** (g + 1)
                    uf, ufr = padb.next()
                    offs = seg_off(p, HC)
                    for (s, lo, hi, si) in pcs:
                        o = offs[si] + (lo - segs[si][1])
                        P.op("act", lambda e, uf=uf, o=o, lo=lo, hi=hi, b=banks[s], s=s: e.activation(
                            out=uf[:, o:o + (hi - lo)], in_=pb[b][:, lo - s * SUB:hi - s * SUB], func=AF.Identity),
                            writes=[("pb", banks[s]), ufr])
                    masked_prefix(uf, ufr, HC, 192 + g * HC, l)
                    dt_, dtr = dbf.next()
                    for si, (kind, c0, n) in enumerate(segs):
                        o = offs[si]
                        a = o - HC
                        end = o + n
                        src, srcr = uf, ufr
                        for m in range(g + 1):
                            sh = 2 ** m
                            lo_i = a + 2 ** (m + 1) - 1
                            dst, dstr = ptmp.next()
                            P.op("dve", lambda e, src=src, dst=dst, lo_i=lo_i, end=end, sh=sh: e.tensor_tensor(
                                out=dst[:, lo_i:end], in0=src[:, lo_i:end], in1=src[:, lo_i - sh:end - sh], op=ALU.add),
                                reads=[srcr], writes=[dstr])
                            src, srcr = dst, dstr
                        P.op("dve", lambda e, src=src, o=o, n=n, c0=c0, uf=uf, dt_=dt_, w=w: e.scalar_tensor_tensor(
                            out=dt_[:, c0:c0 + n], in0=src[:, o:o + n], scalar=1.0 / w, in1=uf[:, o:o + n],
                            op0=ALU.mult, op1=ALU.subtract),
                            reads=[srcr, ufr], writes=[dtr])
                        if p == 0 and kind == 0:
                            oo = o + HALO
                            cc = c0 + HALO
                            ib = _off["icnt"] + g * 16
                            tmpt, tmpr = ptmp.next()
                            P.op("dve", lambda e, src=src, oo=oo, ib=ib, tmpt=tmpt: e.tensor_tensor(
                                out=tmpt[:, 0:16], in0=src[:, oo:oo + 16], in1=prm[:, ib:ib + 16], op=ALU.mult),
                                reads=[srcr, "prm"], writes=[tmpr])
                            P.op("dve", lambda e, oo=oo, cc=cc, uf=uf, tmpt=tmpt, dt_=dt_: e.tensor_tensor(
                                out=dt_[:, cc:cc + 16], in0=tmpt[:, 0:16], in1=uf[:, oo:oo + 16], op=ALU.subtract),
                                reads=[tmpr, ufr], writes=[dtr])
                    carry_update(uf, ufr, HC, 192 + g * HC, l)
                    def pool_mm(g=g, dt_=dt_, dtr=dtr):
                        for s in range(2):
                            b = xB.next()[0]
                            cs = slice(s * SUB, (s + 1) * SUB)
                            P.op("pe", lambda e, b=b, cs=cs: e.matmul(pb[b][:, 0:SUB], lhsT=pw_sb[:, l * 4 + g, :], rhs=dt_[:, cs],
                                                                      start=True, stop=True),
                                 reads=["pw", dtr], writes=[("pb", b)])
                            P.op("act", lambda e, b=b, cs=cs: e.activation(out=ycat[:, 12 + g, cs], in_=pb[b][:, 0:SUB], func=AF.Identity,
                                                                           scale=pcol(("psc", l), g)),
                                 reads=["prm"], writes=[("pb", b), ("yc", 12 + g)])
                    pend_pe.append(pool_mm)
            while pend_pe:
                pend_pe.pop(0)()
            act_preload(AF.Sqrt)
            bs = (xB.next()[0], xB.next()[0])
            ms = (4, 5)
            for c in range(6):
                sq, sqr = sqb.next()
                P.op("act", lambda e, c=c, sq=sq: e.activation(out=sq[:], in_=convb[:, c, :], func=AF.Square),
                     reads=[("cb", c)], writes=[sqr])
                for s in range(2):
                    cs = slice(s * SUB, (s + 1) * SUB)
                    P.op("pe", lambda e, c=c, s=s, sq=sq, cs=cs: e.matmul(pb[bs[s]][:, 0:SUB], lhsT=ones[:], rhs=sq[:, cs],
                                                                          start=(c == 0), stop=(c == 5)),
                         reads=["ones", sqr], writes=[("pb", bs[s])])
                    P.op("pe", lambda e, c=c, s=s, cs=cs: e.matmul(pb[ms[s]][:, 0:SUB], lhsT=ones[:], rhs=convb[:, c, cs],
                                                                   start=(c == 0), stop=(c == 5)),
                         reads=["ones", ("cb", c)], writes=[("pb", ms[s])])
            ln_steps = []

            def ln_post():
                for s in range(2):
                    cs = slice(s * SUB, (s + 1) * SUB)
                    P.op("act", lambda e, s=s, cs=cs: e.activation(out=lnm[:, cs], in_=pb[ms[s]][:, 0:SUB], func=AF.Identity, scale=1.0 / 768),
                         writes=[("pb", ms[s]), ("lnm", s)])
                    P.op("dve", lambda e, s=s, cs=cs: e.tensor_tensor(out=lnt[:, cs], in0=lnm[:, cs], in1=lnm[:, cs], op=ALU.mult),
                         reads=[("lnm", s)], writes=[("lnt", s)])
                    P.op("dve", lambda e, s=s, cs=cs: e.scalar_tensor_tensor(out=lnt[:, cs], in0=pb[bs[s]][:, 0:SUB], scalar=1.0 / 768,
                                                                            in1=lnt[:, cs], op0=ALU.mult, op1=ALU.subtract),
                         writes=[("pb", bs[s]), ("lnt", s)])
                    P.op("act", lambda e, s=s, cs=cs: e.activation(out=lnr[:, cs], in_=lnt[:, cs], func=AF.Sqrt, scale=1.0, bias=epst[:, 0:1]),
                         reads=["epst", ("lnt", s)], writes=[("lnr", s)])
                    P.op("dve", lambda e, s=s, cs=cs: e.reciprocal(out=lnr[:, cs], in_=lnr[:, cs]), writes=[("lnr", s)])
                act_preload(AF.Silu)

            def ln_norm(j):
                P.op("dve", lambda e, j=j: e.tensor_tensor(out=convb[:, j, :], in0=convb[:, j, :], in1=lnm[:], op=ALU.subtract),
                     reads=[("lnm", 0), ("lnm", 1)], writes=[("cb", j)])
                P.op("dve", lambda e, j=j: e.tensor_tensor(out=convb[:, j, :], in0=convb[:, j, :], in1=lnr[:], op=ALU.mult),
                     reads=[("lnr", 0), ("lnr", 1)], writes=[("cb", j)])
                P.op("act", lambda e, j=j: e.activation(out=ycat[:, 6 + j, :], in_=convb[:, j, :], func=AF.Silu,
                                                       scale=pcol(("lng", l), j), bias=pcol(("lnb", l), j)),
                     reads=[("cb", j), "prm"], writes=[("yc", 6 + j)])

            ln_steps.append(ln_post)
            for j in range(6):
                ln_steps.append(lambda j=j: ln_norm(j))
            ln_steps.append(lambda: act_preload(AF.Sqrt))
            KA = [0, 1, 2, 3, 4, 5, 12, 13, 14, 15]
            KBL = [6, 7, 8, 9, 10, 11]
            tile_i = 0
            for hb in range(8):
                h = hctr[0] % 4
                hctr[0] += 1
                P.dma("pool", lambda e, h=h, hb=hb: e.dma_start(out=halves[h][:, 0:6, :], in_=wov[:, 0:6, hb * 256:(hb + 1) * 256]),
                      ("h", h), writes=[("h", h)])
                P.dma("pool", lambda e, h=h, hb=hb: e.dma_start(out=halves[h][:, 6:10, :], in_=wov[:, 12:16, hb * 256:(hb + 1) * 256]),
                      ("h", h), writes=[("h", h)])
                for t in range(2):
                    d = hb * 2 + t
                    for s in range(2):
                        b = allB.next()[0]
                        cs = slice(s * SUB, (s + 1) * SUB)
                        for i, kc in enumerate(KA):
                            P.op("pe", lambda e, i=i, kc=kc, h=h, t=t, cs=cs, b=b: e.matmul(
                                pb[b][:, 0:SUB], lhsT=halves[h][:, i, t * 128:(t + 1) * 128], rhs=ycat[:, kc, cs],
                                start=(i == 0), stop=(i == len(KA) - 1)),
                                reads=[("h", h), ("yc", kc)], writes=[("pb", b)])
                        P.op("dve", lambda e, d=d, cs=cs, b=b: e.tensor_tensor(out=resid[:, d, cs], in0=pb[b][:, 0:SUB],
                                                                              in1=resid[:, d, cs], op=ALU.add),
                             writes=[("pb", b), ("r", d, s)])
                        tile_i += 1
                        if tile_i % 4 == 2 and ln_steps:
                            ln_steps.pop(0)()
            while ln_steps:
                ln_steps.pop(0)()
            for hb in range(8):
                h = hctr[0] % 4
                hctr[0] += 1
                P.dma("pool", lambda e, h=h, hb=hb: e.dma_start(out=halves[h][:, 0:6, :], in_=wov[:, 6:12, hb * 256:(hb + 1) * 256]),
                      ("h", h), writes=[("h", h)])
                for t in range(2):
                    d = hb * 2 + t
                    for s in range(2):
                        b = dnB.next()[0]
                        cs = slice(s * SUB, (s + 1) * SUB)
                        for i, kc in enumerate(KBL):
                            P.op("pe", lambda e, i=i, kc=kc, h=h, t=t, cs=cs, b=b: e.matmul(
                                pb[b][:, 0:SUB], lhsT=halves[h][:, i, t * 128:(t + 1) * 128], rhs=ycat[:, kc, cs],
                                start=(i == 0), stop=(i == len(KBL) - 1)),
                                reads=[("h", h), ("yc", kc)], writes=[("pb", b)])
                        P.op("dve", lambda e, d=d, cs=cs, b=b: e.tensor_tensor(out=resid[:, d, cs], in0=pb[b][:, 0:SUB],
                                                                              in1=resid[:, d, cs], op=ALU.add),
                             writes=[("pb", b), ("r", d, s)])
                    rms_sq(d, lag=2)

        for p in range(DBG.get("passes", NP)):
            load_x(p)
            for kc in range(KC):
                rms_sq(kc)
            for l in range(DBG.get("layers", L)):
                if DBG.get("ffn1", True):
                    rmsnorm(("n1", l), AF.Silu)
                    ffn(l, w_d["ffn1_wg"], w_d["ffn1_wu"], w_d["ffn1_wd"])
                if DBG.get("mixer", True):
                    rmsnorm(("nm", l), AF.Sigmoid)
                    mixer(p, l)
                    if DBG.get("dump") and (p, l) == tuple(DBG["dump"]):
                        dbg_d = nc.dram_tensor("dbg", [128, KC, TT], F32, kind="ExternalOutput").ap()
                        P.dma("pool", lambda e: e.dma_start(out=dbg_d, in_=ycat[:]), "dbg", reads=[("yc", k) for k in range(KC)])
                if DBG.get("ffn2", True):
                    rmsnorm(("n2", l), AF.Silu)
                    ffn(l, w_d["ffn2_wg"], w_d["ffn2_wu"], w_d["ffn2_wd"])
            final_norm_store(p, p == NP - 1)
        P.dma("sp", lambda e: e.dma_start(out=nb_d, in_=car[:]), "nb", reads=["car"])
        P.emit(nc, final_waits=[("y", "sqb", i) for i in range(3)]
               + [("y", "zev", i) for i in range(6)] + ["nb", "dbg"])
    return nc


def _fm(v, nchunk):
    return np.ascontiguousarray(np.asarray(v, np.float32).reshape(nchunk, 128).T)


def _pack_prm(inp, c):
    prm = np.zeros((128, NPRM), np.float32)

    def put(name, arr):
        arr = np.asarray(arr, np.float32).reshape(128, -1)
        prm[:, _off[name]:_off[name] + arr.shape[1]] = arr

    for l in range(L):
        put(("n1", l), _fm(inp["ffn1_norm"][l], KC))
        put(("nm", l), _fm(inp["mix_norm"][l], KC))
        put(("n2", l), _fm(inp["ffn2_norm"][l], KC))
        put(("caw", l), np.asarray(inp["conv_a_w"][l]).reshape(3, 6, 128).transpose(2, 1, 0))
        put(("cbw", l), np.asarray(inp["conv_b_w"][l]).reshape(31, 6, 128).transpose(2, 1, 0))
        put(("cbb", l), _fm(inp["conv_b_bias"][l], 6))
        put(("lng", l), _fm(inp["ln_b_gain"][l], 6))
        put(("lnb", l), _fm(inp["ln_b_bias"][l], 6))
        put(("psc", l), _fm(inp["pool_scale"][l], 4))
    put("nf", _fm(inp["final_norm"], KC))
    q = c % 4
    prm[:, _off["hmask"]] = 0.0 if q == 0 else 1.0
    ic = np.zeros((4, 16), np.float32)
    for g in range(4):
        w = 2 ** (g + 1)
        for i in range(16):
            ic[g, i] = (1.0 / min(i + 1, w)) if q == 0 else (1.0 / w)
    prm[:, _off["icnt"]:_off["icnt"] + 64] = ic.reshape(1, 64)
    cache = np.zeros((128, L, CARW), np.float32)
    for l in range(L):
        cache[:, l, 0:12] = np.asarray(inp["cache_conv_a"][l, c]).reshape(HA, 6, 128).transpose(2, 1, 0).reshape(128, 12)
        cache[:, l, 12:192] = np.asarray(inp["cache_conv_b"][l, c]).reshape(HB, 6, 128).transpose(2, 1, 0).reshape(128, 180)
        cache[:, l, 192:252] = np.asarray(inp["cache_pool"][l, c]).reshape(HC, 4, 128).transpose(2, 1, 0).reshape(128, 60)
    prm[:, _off["cache"]:_off["cache"] + L * CARW] = cache.reshape(128, L * CARW)
    return prm


def kernel(**inputs):
    inp = {k: np.asarray(v) for k, v in inputs.items()}
    xp = inp["x_prompt"].astype(np.float32, copy=False)
    xs = inp["x_sample"].astype(np.float32, copy=False)
    B, S, _ = xp.shape
    QL = S // 4
    wnames = ("ffn1_wg", "ffn1_wu", "ffn1_wd", "w_in", "pool_w", "w_out", "ffn2_wg", "ffn2_wu", "ffn2_wd")
    wts = {k: np.ascontiguousarray(inp[k], dtype=np.float32) for k in wnames}
    in_maps = []
    for c in range(NCORES):
        b, q = c // 4, c % 4
        halo = np.zeros((HALO, D), np.float32) if q == 0 else xp[b, q * QL - HALO:q * QL]
        toks = np.concatenate([xs[c], halo, xp[b, q * QL:(q + 1) * QL]], axis=0)
        xT = np.ascontiguousarray(toks.reshape(NP, TT, KC, 128).transpose(3, 0, 2, 1))
        m = {"xT": xT, "prm": _pack_prm(inp, c)}
        m.update(wts)
        in_maps.append(m)
    nc = build_nc()
    res = run_bass_kernel_spmd(nc, in_maps, core_ids=list(range(NCORES)))
    y_prompt = np.zeros((B, S, D), np.float32)
    y_sample = np.zeros(xs.shape, np.float32)
    na_p = np.zeros((L, B, HA, 768), np.float32)
    nb_p = np.zeros((L, B, HB, 768), np.float32)
    np_p = np.zeros((L, B, HC, 512), np.float32)
    na_s = np.zeros((L, NCORES, HA, 768), np.float32)
    nb_s = np.zeros((L, NCORES, HB, 768), np.float32)
    np_s = np.zeros((L, NCORES, HC, 512), np.float32)
    for c in range(NCORES):
        b, q = c // 4, c % 4
        r = res.results[c]
        toks = np.asarray(r["yT"]).transpose(1, 3, 2, 0).reshape(NP * TT, D)
        y_sample[c] = toks[0:SAMP]
        y_prompt[b, q * QL:(q + 1) * QL] = toks[SAMP + HALO:]
        nbuf = np.asarray(r["nbuf"]).reshape(128, 2, L, CARW)
        for l in range(L):
            for kind, (da, db, dp, idx) in ((1, (na_s, nb_s, np_s, c)), (0, (na_p, nb_p, np_p, b))):
                if kind == 0 and q != 3:
                    continue
                v = nbuf[:, kind, l]
                da[l, idx] = v[:, 0:12].reshape(128, 6, HA).transpose(2, 1, 0).reshape(HA, 768)
                db[l, idx] = v[:, 12:192].reshape(128, 6, HB).transpose(2, 1, 0).reshape(HB, 768)
                dp[l, idx] = v[:, 192:252].reshape(128, 4, HC).transpose(2, 1, 0).reshape(HC, 512)
    return (y_prompt, y_sample, na_p, nb_p, np_p, na_s, nb_s, np_s)
```

```python
import numpy as np
from contextlib import ExitStack
import concourse.bass as bass
import concourse.mybir as mybir
from concourse.bass_utils import run_bass_kernel_spmd

F32 = mybir.dt.float32
BF16 = mybir.dt.bfloat16
ALU = mybir.AluOpType
AF = mybir.ActivationFunctionType

D = 2048
KC = 16
FF = 5632
NF = 44
GF = 2
NG = NF // GF
DIN = 4352
L = 2
TT = 544
SUB = 272
NP = 4
HALO = 64
SAMP = 64
HA, HB, HC = 2, 30, 15
CARW = 6 * HA + 6 * HB + 4 * HC
EPS = 1e-6
NCORES = 8
DBG = {}
SELF_SYNC_WINDOW = 6

_off = {}
_n = 0


def _alloc(name, n):
    global _n
    _off[name] = _n
    _n += n


for _l in range(L):
    _alloc(("n1", _l), KC)
    _alloc(("nm", _l), KC)
    _alloc(("n2", _l), KC)
    _alloc(("caw", _l), 6 * 3)
    _alloc(("cbw", _l), 6 * 31)
    _alloc(("cbb", _l), 6)
    _alloc(("lng", _l), 6)
    _alloc(("lnb", _l), 6)
    _alloc(("psc", _l), 4)
_alloc("nf", KC)
_alloc("hmask", 1)
_alloc("icnt", 4 * 16)
_alloc("cache", L * CARW)
NPRM = _n

ENGS = ("pe", "act", "dve", "pool", "sp")


class Op:
    __slots__ = ("eng", "fn", "reads", "writes", "dma", "key", "idx", "deps", "milestone", "mval")

    def __init__(self, eng, fn, reads, writes, dma, key):
        self.eng = eng
        self.fn = fn
        self.reads = reads
        self.writes = writes
        self.dma = dma
        self.key = key
        self.deps = None
        self.milestone = False
        self.mval = 0


class Prog:
    def __init__(self):
        self.ops = []

    def op(self, eng, fn, reads=(), writes=()):
        self.ops.append(Op(eng, fn, tuple(reads), tuple(writes), False, None))

    def dma(self, eng, fn, key, reads=(), writes=()):
        self.ops.append(Op(eng, fn, tuple(reads), tuple(writes), True, key))

    def analyze(self):
        last_w = {}
        readers = {}
        ops = self.ops
        epos = {}
        pos = [0] * len(ops)
        for i, o in enumerate(ops):
            pos[i] = epos.get(o.eng, 0)
            epos[o.eng] = pos[i] + 1
        for i, o in enumerate(ops):
            o.idx = i
            deps = set()
            for r in o.reads:
                w = last_w.get(r)
                if w is not None:
                    deps.add(w)
            for w_ in o.writes:
                w = last_w.get(w_)
                if w is not None:
                    deps.add(w)
                rs = readers.get(w_)
                if rs:
                    deps.update(rs)
            deps.discard(i)
            keep = set()
            for d in deps:
                p = ops[d]
                if p.dma:
                    if o.dma and o.key == p.key:
                        continue
                    keep.add(d)
                elif p.eng == o.eng and not o.dma:
                    if o.eng != "pe" and pos[i] - pos[d] <= SELF_SYNC_WINDOW:
                        keep.add(d)
                    continue
                else:
                    keep.add(d)
            o.deps = keep
            for w_ in o.writes:
                last_w[w_] = i
                readers[w_] = []
            for r in o.reads:
                readers.setdefault(r, []).append(i)
        for o in ops:
            for d in o.deps:
                ops[d].milestone = True
        cnt = {}
        for o in ops:
            if o.dma:
                k = ("dma", o.key)
                cnt[k] = cnt.get(k, 0) + 16
                o.mval = cnt[k]
            elif o.milestone:
                k = ("eng", o.eng)
                cnt[k] = cnt.get(k, 0) + 1
                o.mval = cnt[k]
        self.sem_final = cnt

    def emit(self, nc, final_waits=()):
        self.analyze()
        ops = self.ops
        with ExitStack() as es:
            sems = {}
            for k in self.sem_final:
                nm = "s_" + "_".join(str(x) for x in (k[1] if isinstance(k[1], tuple) else (k[1],)))
                sems[k] = es.enter_context(nc.semaphore(nm))
            block = es.enter_context(nc.Block())
            per_eng = {e: [] for e in ENGS}
            for o in ops:
                per_eng[o.eng].append(o)

            def run(engine_obj, ename):
                waited = {}
                for o in per_eng[ename]:
                    need = {}
                    for d in o.deps:
                        p = ops[d]
                        k = ("dma", p.key) if p.dma else ("eng", p.eng)
                        if p.mval > need.get(k, 0):
                            need[k] = p.mval
                    for k, v in need.items():
                        if waited.get(k, 0) >= v:
                            continue
                        engine_obj.wait_ge(sems[k], v)
                        waited[k] = v
                    ins = o.fn(engine_obj)
                    if o.dma:
                        ins.then_inc(sems[("dma", o.key)], 16)
                    elif o.milestone:
                        ins.then_inc(sems[("eng", o.eng)], 1)
                if ename == "sp":
                    for key in final_waits:
                        k = ("dma", key)
                        if k in sems:
                            engine_obj.wait_ge(sems[k], self.sem_final[k])

            @block.tensor
            def _(e):
                run(e, "pe")

            @block.scalar
            def _(e):
                run(e, "act")

            @block.vector
            def _(e):
                run(e, "dve")

            @block.gpsimd
            def _(e):
                run(e, "pool")

            @block.sync
            def _(e):
                run(e, "sp")


class Rot:
    def __init__(self, tiles, name):
        self.tiles = tiles
        self.name = name
        self.i = 0

    def next(self):
        k = self.i % len(self.tiles)
        self.i += 1
        return self.tiles[k], (self.name, k)


def segs_of(p):
    if p == 0:
        return [(1, 0, SAMP), (0, SAMP, TT - SAMP)]
    return [(0, 0, TT)]


def seg_off(p, H):
    offs = []
    o = 0
    for (_, _, n) in segs_of(p):
        offs.append(o + H)
        o += H + n
    return offs


def pieces(p):
    out = []
    for s in range(2):
        a, b = s * SUB, (s + 1) * SUB
        for si, (_, c0, n) in enumerate(segs_of(p)):
            lo, hi = max(a, c0), min(b, c0 + n)
            if lo < hi:
                out.append((s, lo, hi, si))
    return out


def build_nc():
    nc = bass.Bass("TRN2", target_bir_lowering=False)
    dt = lambda name, shape, kind: nc.dram_tensor(name, shape, F32, kind=kind).ap()
    xT = dt("xT", [128, NP, KC, TT], "ExternalInput")
    prm_d = dt("prm", [128, NPRM], "ExternalInput")
    w_d = {}
    for nm, shp in (("ffn1_wg", [L, D, FF]), ("ffn1_wu", [L, D, FF]), ("ffn1_wd", [L, FF, D]),
                    ("w_in", [L, D, DIN]), ("pool_w", [L, 4, 128, 128]), ("w_out", [L, D, D]),
                    ("ffn2_wg", [L, D, FF]), ("ffn2_wu", [L, D, FF]), ("ffn2_wd", [L, FF, D])):
        w_d[nm] = dt(nm, shp, "ExternalInput")
    yT = dt("yT", [128, NP, KC, TT], "ExternalOutput")
    nb_d = dt("nbuf", [128, 2 * L * CARW], "ExternalOutput")

    P = Prog()
    with ExitStack() as es:
        sb = lambda n, s, d=F32: es.enter_context(nc.sbuf_tensor(n, s, d))
        resid = sb("resid", [128, KC, TT])
        xn = sb("xn", [128, KC, TT], BF16)
        ycat = sb("ycat", [128, KC, TT], BF16)
        convb = sb("convb", [128, 6, TT])
        hid = [sb(f"hid{i}", [128, GF, TT], BF16) for i in range(2)]
        halves = [sb(f"wh{i}", [128, KC, 256], BF16) for i in range(4)]
        wds = [sb(f"wd{i}", [128, GF, D], BF16) for i in range(2)]
        PADW = TT + 2 * HB
        sqb = Rot([sb(f"sqb{i}", [128, TT]) for i in range(3)], "sqb")
        sgb = Rot([sb(f"sg{i}", [128, SUB]) for i in range(4)], "sg")
        zev = Rot([sb(f"zev{i}", [128, TT]) for i in range(6)], "zev")
        padb = Rot([sb(f"pad{i}", [128, PADW]) for i in range(3)], "pad")
        ptmp = Rot([sb(f"ptmp{i}", [128, PADW]) for i in range(2)], "ptmp")
        dbf = Rot([sb(f"dbf{i}", [128, TT], BF16) for i in range(2)], "dbf")
        rstd = sb("rstd", [128, TT])
        sqacc = [sb(f"sqacc{i}", [128, TT]) for i in range(4)]
        lnm = sb("lnm", [128, TT])
        lnr = sb("lnr", [128, TT])
        lnt = sb("lnt", [128, TT])
        prm = sb("prm_sb", [128, NPRM])
        car = sb("car", [128, 2 * L * CARW])
        pw_sb = sb("pw_sb", [128, L * 4, 128], BF16)
        ones = sb("ones", [128, 128])
        epst = sb("epst", [128, 1])
        dmy = sb("dmy", [128, 4])
        pb = [es.enter_context(nc.psum_tensor(f"pb{i}", [128, 512], F32)) for i in range(8)]
        guB = Rot(list(range(0, 4)), "guB")
        dnB = Rot(list(range(4, 8)), "dnB")
        zB = Rot(list(range(0, 6)), "zB")
        allB = Rot(list(range(0, 8)), "allB")
        tailB = Rot(list(range(0, 6)), "tailB")
        xB = Rot(list(range(6, 8)), "xB")
        hctr = [0]
        wdctr = [0]

        def pcol(name, i=0):
            o = _off[name] + i
            return prm[:, o:o + 1]

        def car_sl(kind, l, o, n):
            base = (kind * L + l) * CARW + o
            return car[:, base:base + n]

        P.dma("sp", lambda e: e.dma_start(out=prm[:], in_=prm_d), "prm", writes=["prm"])
        P.dma("pool", lambda e: e.dma_start(out=pw_sb[:], in_=w_d["pool_w"].rearrange("l g c d -> c (l g) d")),
              "prm2", writes=["pw"])
        P.op("dve", lambda e: e.memset(ones[:], 1.0), writes=["ones"])
        P.op("dve", lambda e: e.memset(epst[:], EPS), writes=["epst"])
        P.op("dve", lambda e: e.memset(car[:, 0:L * CARW], 0.0), writes=["car"])
        co = _off["cache"]
        P.op("dve", lambda e: e.tensor_copy(out=car[:, L * CARW:2 * L * CARW], in_=prm[:, co:co + L * CARW]),
             reads=["prm"], writes=["car"])

        def load_x(p):
            for q in range(8):
                P.dma("sp", lambda e, q=q: e.dma_start(out=resid[:, 2 * q:2 * q + 2, :], in_=xT[:, p, 2 * q:2 * q + 2, :]),
                      ("x", q), writes=[("r", kc, s) for kc in range(2 * q, 2 * q + 2) for s in range(2)])

        def sumsq_rstd(src_fn, nchunks, src_res_fn, out_tile, out_res, scale, mean_tile=None):
            b0, b1 = xB.next()[0], xB.next()[0]
            bs = (b0, b1)
            if mean_tile is not None:
                ms = (4, 5)
            for c in range(nchunks):
                sq, sqr = sqb.next()
                P.op("act", lambda e, c=c, sq=sq: e.activation(out=sq[:], in_=src_fn(c), func=AF.Square),
                     reads=src_res_fn(c), writes=[sqr])
                for s in range(2):
                    P.op("pe", lambda e, c=c, s=s, sq=sq: e.matmul(pb[bs[s]][:, 0:SUB], lhsT=ones[:], rhs=sq[:, s * SUB:(s + 1) * SUB],
                                                                  start=(c == 0), stop=(c == nchunks - 1)),
                         reads=["ones", sqr], writes=[("pb", bs[s])])
                    if mean_tile is not None:
                        P.op("pe", lambda e, c=c, s=s: e.matmul(pb[ms[s]][:, 0:SUB], lhsT=ones[:], rhs=src_fn(c)[:, s * SUB:(s + 1) * SUB],
                                                               start=(c == 0), stop=(c == nchunks - 1)),
                             reads=["ones"] + src_res_fn(c), writes=[("pb", ms[s])])
            for s in range(2):
                cs = slice(s * SUB, (s + 1) * SUB)
                if mean_tile is None:
                    P.op("act", lambda e, s=s, cs=cs: e.activation(out=out_tile[:, cs], in_=pb[bs[s]][:, 0:SUB], func=AF.Sqrt,
                                                                  scale=scale, bias=epst[:, 0:1]),
                         reads=["epst"], writes=[("pb", bs[s]), (out_res, s)])
                else:
                    P.op("act", lambda e, s=s, cs=cs: e.activation(out=mean_tile[:, cs], in_=pb[ms[s]][:, 0:SUB], func=AF.Identity,
                                                                  scale=scale),
                         writes=[("pb", ms[s]), ("lnm", s)])
                    P.op("dve", lambda e, s=s, cs=cs: e.tensor_tensor(out=lnt[:, cs], in0=mean_tile[:, cs], in1=mean_tile[:, cs], op=ALU.mult),
                         reads=[("lnm", s)], writes=[("lnt", s)])
                    P.op("dve", lambda e, s=s, cs=cs: e.scalar_tensor_tensor(out=lnt[:, cs], in0=pb[bs[s]][:, 0:SUB], scalar=scale,
                                                                            in1=lnt[:, cs], op0=ALU.mult, op1=ALU.subtract),
                         writes=[("pb", bs[s]), ("lnt", s)])
                    P.op("act", lambda e, s=s, cs=cs: e.activation(out=out_tile[:, cs], in_=lnt[:, cs], func=AF.Sqrt,
                                                                  scale=1.0, bias=epst[:, 0:1]),
                         reads=["epst", ("lnt", s)], writes=[(out_res, s)])
                P.op("dve", lambda e, s=s, cs=cs: e.reciprocal(out=out_tile[:, cs], in_=out_tile[:, cs]),
                     writes=[(out_res, s)])

        _pl = {AF.Sqrt: 0, AF.Silu: 1, AF.Sigmoid: 2}

        def act_preload(func):
            i = _pl[func]
            P.op("act", lambda e: e.activation(out=dmy[:, i:i + 1], in_=epst[:, 0:1], func=func),
                 reads=["epst"], writes=[("dmy", i)])

        pending_adds = []
        stat_state = {"bs": None, "done": 0, "ready": []}

        def _emit_add(q2, sq2, sqr2, kc2):
            P.op("dve", lambda e, q2=q2, sq2=sq2: e.tensor_tensor(out=sqacc[q2][:], in0=sqacc[q2][:], in1=sq2[:], op=ALU.add),
                 reads=[sqr2], writes=[("sqacc", q2)])
            if kc2 % 4 == 3:
                stat_state["ready"].append([q2, 0])

        def _emit_stat(q):
            if stat_state["bs"] is None:
                stat_state["bs"] = (xB.next()[0], xB.next()[0])
            bs = stat_state["bs"]
            for s in range(2):
                P.op("pe", lambda e, q=q, s=s: e.matmul(pb[bs[s]][:, 0:SUB], lhsT=ones[:], rhs=sqacc[q][:, s * SUB:(s + 1) * SUB],
                                                       start=(q == 0), stop=(q == 3)),
                     reads=["ones", ("sqacc", q)], writes=[("pb", bs[s])])
            stat_state["done"] += 1

        def rms_sq(kc, lag=0):
            q, r = kc // 4, kc % 4
            if r == 0:
                P.op("act", lambda e: e.activation(out=sqacc[q][:], in_=resid[:, kc, :], func=AF.Square),
                     reads=[("r", kc, 0), ("r", kc, 1)], writes=[("sqacc", q)])
            else:
                sq, sqr = sqb.next()
                P.op("act", lambda e: e.activation(out=sq[:], in_=resid[:, kc, :], func=AF.Square),
                     reads=[("r", kc, 0), ("r", kc, 1)], writes=[sqr])
                pending_adds.append((q, sq, sqr, kc))
            while stat_state["ready"] and stat_state["ready"][0][1] >= 1 and lag > 0:
                _emit_stat(stat_state["ready"].pop(0)[0])
            for ent in stat_state["ready"]:
                ent[1] += 1
            while len(pending_adds) > lag:
                _emit_add(*pending_adds.pop(0))

        def rms_flush():
            while pending_adds:
                _emit_add(*pending_adds.pop(0))

        def rms_finish(next_func=None):
            rms_flush()
            while stat_state["ready"]:
                _emit_stat(stat_state["ready"].pop(0)[0])
            assert stat_state["done"] == 4, stat_state
            bs = stat_state["bs"]
            stat_state["bs"] = None
            stat_state["done"] = 0
            for s in range(2):
                cs = slice(s * SUB, (s + 1) * SUB)
                P.op("act", lambda e, s=s, cs=cs: e.activation(out=rstd[:, cs], in_=pb[bs[s]][:, 0:SUB], func=AF.Sqrt,
                                                              scale=1.0 / D, bias=epst[:, 0:1]),
                     reads=["epst"], writes=[("pb", bs[s]), ("rstd", s)])
                P.op("dve", lambda e, s=s, cs=cs: e.reciprocal(out=rstd[:, cs], in_=rstd[:, cs]),
                     writes=[("rstd", s)])
            if next_func is not None:
                act_preload(next_func)

        def rmsnorm(gname, next_func=None):
            rms_finish(next_func)
            for kc in range(KC):
                P.op("dve", lambda e, kc=kc: e.scalar_tensor_tensor(out=xn[:, kc, :], in0=resid[:, kc, :], scalar=pcol(gname, kc),
                                                                   in1=rstd[:], op0=ALU.mult, op1=ALU.mult),
                     reads=[("r", kc, 0), ("r", kc, 1), ("rstd", 0), ("rstd", 1), "prm"], writes=[("xn", kc)])

        def final_norm_store(p, last):
            rms_finish()
            stage = ([(sqb.tiles[i], ("sqb", i)) for i in range(3)] + [(zev.tiles[i], ("zev", i)) for i in range(6)])
            for kc in range(KC):
                yt, ytr = stage[kc % len(stage)]
                P.op("dve", lambda e, kc=kc, yt=yt: e.scalar_tensor_tensor(out=yt[:], in0=resid[:, kc, :], scalar=pcol("nf", kc),
                                                                          in1=rstd[:], op0=ALU.mult, op1=ALU.mult),
                     reads=[("r", kc, 0), ("r", kc, 1), ("rstd", 0), ("rstd", 1), "prm"], writes=[ytr])
                P.dma("sp", lambda e, kc=kc, yt=yt: e.dma_start(out=yT[:, p, kc, :], in_=yt[:]), ("y",) + ytr, reads=[ytr])

        def ffn(l, wg, wu, wd):
            wgv = wg[l].rearrange("(kc p) f -> p kc f", p=128)
            wuv = wu[l].rearrange("(kc p) f -> p kc f", p=128)
            wdv = wd[l].rearrange("(fc p) d -> p fc d", p=128)
            slots = {}

            def dn_tile(gg, d, s, rot=None):
                hg, hu, wdi = slots[gg]
                b = (rot or dnB).next()[0]
                cs = slice(s * SUB, (s + 1) * SUB)
                for fi in range(GF):
                    P.op("pe", lambda e, fi=fi: e.matmul(pb[b][:, 0:SUB], lhsT=wds[wdi][:, fi, d * 128:(d + 1) * 128],
                                                        rhs=hid[gg % 2][:, fi, cs], start=(fi == 0), stop=(fi == GF - 1)),
                         reads=[("wd", wdi), ("hd", gg % 2, fi, s)], writes=[("pb", b)])
                P.op("dve", lambda e: e.scalar_tensor_tensor(out=resid[:, d, cs], in0=pb[b][:, 0:SUB], scalar=0.5,
                                                            in1=resid[:, d, cs], op0=ALU.mult, op1=ALU.add),
                     writes=[("pb", b), ("r", d, s)])

            for g in range(NG + 1):
                gu_tiles = []
                if g < NG:
                    hg = hctr[0] % 4
                    hu = (hctr[0] + 1) % 4
                    hctr[0] += 2
                    wdi = wdctr[0] % 2
                    wdctr[0] += 1
                    slots[g] = (hg, hu, wdi)
                    f0 = g * GF * 128
                    P.dma("pool", lambda e, hg=hg, f0=f0: e.dma_start(out=halves[hg][:], in_=wgv[:, :, f0:f0 + 256]),
                          ("h", hg), writes=[("h", hg)])
                    P.dma("pool", lambda e, hu=hu, f0=f0: e.dma_start(out=halves[hu][:], in_=wuv[:, :, f0:f0 + 256]),
                          ("h", hu), writes=[("h", hu)])
                    P.dma("pool", lambda e, wdi=wdi, g=g: e.dma_start(out=wds[wdi][:], in_=wdv[:, g * GF:(g + 1) * GF, :]),
                          ("wd", wdi), writes=[("wd", wdi)])
                    for fi in range(GF):
                        for gu in range(2):
                            for s in range(2):
                                gu_tiles.append((fi, gu, s))
                dn_tiles = [(d, s) for d in range(KC) for s in range(2)] if g >= 1 else []
                sgmap = {}
                nsteps = max(len(gu_tiles), 8)
                if g == NG:
                    act_preload(AF.Sqrt)
                pre_banks = {}
                if g == 0:
                    for i in range(4):
                        pre_banks[i] = guB.next()[0]
                    for kc in range(KC):
                        for i in range(4):
                            fi, gu, s = gu_tiles[i]
                            hsel = hg if gu == 0 else hu
                            b = pre_banks[i]
                            cs = slice(s * SUB, (s + 1) * SUB)
                            P.op("pe", lambda e, kc=kc, hsel=hsel, fi=fi, cs=cs, b=b: e.matmul(
                                pb[b][:, 0:SUB], lhsT=halves[hsel][:, kc, fi * 128:(fi + 1) * 128], rhs=xn[:, kc, cs],
                                start=(kc == 0), stop=(kc == KC - 1)),
                                reads=[("h", hsel), ("xn", kc)], writes=[("pb", b)])
                for i in range(nsteps):
                    if i < len(gu_tiles):
                        fi, gu, s = gu_tiles[i]
                        hsel = hg if gu == 0 else hu
                        cs = slice(s * SUB, (s + 1) * SUB)
                        if i in pre_banks:
                            b = pre_banks[i]
                        else:
                            b = guB.next()[0]
                            for kc in range(KC):
                                P.op("pe", lambda e, kc=kc, hsel=hsel, fi=fi, cs=cs, b=b: e.matmul(
                                    pb[b][:, 0:SUB], lhsT=halves[hsel][:, kc, fi * 128:(fi + 1) * 128], rhs=xn[:, kc, cs],
                                    start=(kc == 0), stop=(kc == KC - 1)),
                                    reads=[("h", hsel), ("xn", kc)], writes=[("pb", b)])
                        if gu == 0:
                            sg, sgr = sgb.next()
                            sgmap[(fi, s)] = (sg, sgr)
                            P.op("act", lambda e, sg=sg, b=b: e.activation(out=sg[:], in_=pb[b][:, 0:SUB], func=AF.Silu),
                                 writes=[("pb", b), sgr])
                        else:
                            sg, sgr = sgmap[(fi, s)]
                            P.op("dve", lambda e, sg=sg, b=b, fi=fi, cs=cs, g=g: e.tensor_tensor(
                                out=hid[g % 2][:, fi, cs], in0=pb[b][:, 0:SUB], in1=sg[:], op=ALU.mult),
                                reads=[sgr], writes=[("pb", b), ("hd", g % 2, fi, s)])
                    if dn_tiles:
                        for (d, s) in dn_tiles[4 * i:4 * i + 4]:
                            dn_tile(g - 1, d, s, tailB if g == NG else None)
                            if g == NG and s == 1:
                                rms_sq(d, lag=2)

        def mixer(p, l):
            segs = segs_of(p)
            pcs = pieces(p)
            winv = w_d["w_in"][l].rearrange("(kc p) f -> p kc f", p=128)
            wov = w_d["w_out"][l].rearrange("(kc p) f -> p kc f", p=128)
            Z = []
            for j in range(6):
                Z.append(("gg", j, 3072 + j * 128))
                Z.append(("ga", j, 2304 + j * 128))
                Z.append(("ha", j, 0 + j * 128))
                Z.append(("ca", j, 1536 + j * 128))
                Z.append(("ba", j, 768 + j * 128))
            Z = [("up", g, 3840 + g * 128) for g in range(4)] + Z
            held = {}
            pend_pe = []

            def masked_prefix(buf, bufr, H, caro, l):
                offs = seg_off(p, H)
                if p == 0:
                    o1 = offs[1]
                    P.op("dve", lambda e: e.tensor_scalar(out=buf[:, o1:o1 + HALO], in0=buf[:, o1:o1 + HALO],
                                                         scalar1=pcol("hmask"), scalar2=None, op0=ALU.mult),
                         reads=["prm"], writes=[bufr])
                for si, (kind, c0, n) in enumerate(segs):
                    o = offs[si]
                    P.op("dve", lambda e, o=o, kind=kind: e.tensor_copy(out=buf[:, o - H:o], in_=car_sl(kind, l, caro, H)),
                         reads=["car"], writes=[bufr])

            def carry_update(buf, bufr, H, caro, l):
                offs = seg_off(p, H)
                for si, (kind, c0, n) in enumerate(segs):
                    o = offs[si]
                    P.op("dve", lambda e, o=o, n=n, kind=kind: e.tensor_copy(out=car_sl(kind, l, caro, H), in_=buf[:, o + n - H:o + n]),
                         reads=[bufr], writes=["car"])

            for zi, (typ, j, col) in enumerate(Z):
                if zi % 2 == 0:
                    h = hctr[0] % 4
                    hctr[0] += 1
                    cur_h = h
                    for t in range(2):
                        if zi + t < len(Z):
                            c2 = Z[zi + t][2]
                            P.dma("pool", lambda e, h=h, t=t, c2=c2: e.dma_start(out=halves[h][:, :, t * 128:(t + 1) * 128],
                                                                                 in_=winv[:, :, c2:c2 + 128]),
                                  ("h", h), writes=[("h", h)])
                hh = cur_h
                t = zi % 2
                if zi == 0:
                    pre_banks = {0: [zB.next()[0], zB.next()[0]], 1: [zB.next()[0], zB.next()[0]]}
                    for kc in range(KC):
                        for t2 in range(2):
                            for s in range(2):
                                b = pre_banks[t2][s]
                                cs = slice(s * SUB, (s + 1) * SUB)
                                P.op("pe", lambda e, kc=kc, hh=hh, t2=t2, cs=cs, b=b: e.matmul(
                                    pb[b][:, 0:SUB], lhsT=halves[hh][:, kc, t2 * 128:(t2 + 1) * 128], rhs=xn[:, kc, cs],
                                    start=(kc == 0), stop=(kc == KC - 1)),
                                    reads=[("h", hh), ("xn", kc)], writes=[("pb", b)])
                if zi < 2:
                    banks = pre_banks[zi]
                else:
                    banks = []
                    for s in range(2):
                        b = zB.next()[0]
                        banks.append(b)
                        cs = slice(s * SUB, (s + 1) * SUB)
                        for kc in range(KC):
                            P.op("pe", lambda e, kc=kc, hh=hh, t=t, cs=cs, b=b: e.matmul(
                                pb[b][:, 0:SUB], lhsT=halves[hh][:, kc, t * 128:(t + 1) * 128], rhs=xn[:, kc, cs],
                                start=(kc == 0), stop=(kc == KC - 1)),
                                reads=[("h", hh), ("xn", kc)], writes=[("pb", b)])
                while pend_pe:
                    pend_pe.pop(0)()
                if typ in ("gg", "ga", "ha", "ca", "ba"):
                    zt, ztr = zev.next()
                    fn = AF.Sigmoid if typ == "gg" else AF.Identity
                    for s in range(2):
                        cs = slice(s * SUB, (s + 1) * SUB)
                        P.op("act", lambda e, zt=zt, cs=cs, b=banks[s], fn=fn: e.activation(out=zt[:, cs], in_=pb[b][:, 0:SUB], func=fn),
                             writes=[("pb", banks[s]), ztr])
                    held[typ] = (zt, ztr)
                if typ == "ga":
                    sg_t, sg_r = held["gg"]
                    ga_t, ga_r = held["ga"]
                    vf, vfr = padb.next()
                    offs = seg_off(p, HB)
                    for si, (kind, c0, n) in enumerate(segs):
                        o = offs[si]
                        P.op("dve", lambda e, o=o, c0=c0, n=n, vf=vf, ga_t=ga_t, sg_t=sg_t: e.tensor_tensor(
                            out=vf[:, o:o + n], in0=ga_t[:, c0:c0 + n], in1=sg_t[:, c0:c0 + n], op=ALU.mult),
                            reads=[sg_r, ga_r], writes=[vfr])
                    masked_prefix(vf, vfr, HB, 12 + j * HB, l)
                    carry_update(vf, vfr, HB, 12 + j * HB, l)
                    wb = _off[("cbw", l)] + j * 31
                    NACC = 4
                    for si, (kind, c0, n) in enumerate(segs):
                        o = offs[si] - HB
                        accs = [(lambda c0=c0, n=n, j=j: convb[:, j, c0:c0 + n], [("cb", j)]),
                                (lambda c0=c0, n=n: lnm[:, c0:c0 + n], [("lnm", 0), ("lnm", 1)]),
                                (lambda c0=c0, n=n: lnt[:, c0:c0 + n], [("lnt", 0), ("lnt", 1)]),
                                (lambda c0=c0, n=n: lnr[:, c0:c0 + n], [("lnr", 0), ("lnr", 1)])]
                        for k in range(31):
                            afn, ares = accs[k % NACC]
                            if k == 0:
                                P.op("act", lambda e, o=o, n=n, vf=vf, wb=wb, afn=afn, j=j: e.activation(
                                    out=afn(), in_=vf[:, o:o + n], func=AF.Identity, scale=prm[:, wb:wb + 1],
                                    bias=pcol(("cbb", l), j)),
                                    reads=[vfr, "prm"], writes=ares)
                            elif k < NACC:
                                P.op("act", lambda e, o=o, n=n, vf=vf, wb=wb, afn=afn, k=k: e.activation(
                                    out=afn(), in_=vf[:, o + k:o + k + n], func=AF.Identity, scale=prm[:, wb + k:wb + k + 1]),
                                    reads=[vfr, "prm"], writes=ares)
                            else:
                                P.op("dve", lambda e, o=o, n=n, vf=vf, wb=wb, afn=afn, k=k: e.scalar_tensor_tensor(
                                    out=afn(), in0=vf[:, o + k:o + k + n], scalar=prm[:, wb + k:wb + k + 1],
                                    in1=afn(), op0=ALU.mult, op1=ALU.add),
                                    reads=[vfr, "prm"], writes=ares)
                        P.op("dve", lambda e, a0=accs[0][0], a1=accs[1][0]: e.tensor_tensor(out=a0(), in0=a0(), in1=a1(), op=ALU.add),
                             reads=accs[1][1], writes=accs[0][1])
                        P.op("dve", lambda e, a2=accs[2][0], a3=accs[3][0]: e.tensor_tensor(out=a2(), in0=a2(), in1=a3(), op=ALU.add),
                             reads=accs[3][1], writes=accs[2][1])
                        P.op("dve", lambda e, a0=accs[0][0], a2=accs[2][0]: e.tensor_tensor(out=a0(), in0=a0(), in1=a2(), op=ALU.add),
                             reads=accs[2][1], writes=accs[0][1])
                elif typ == "ca":
                    ha_t, ha_r = held["ha"]
                    ca_t, ca_r = held["ca"]
                    cf, cfr = padb.next()
                    offs = seg_off(p, HA)
                    for si, (kind, c0, n) in enumerate(segs):
                        o = offs[si]
                        P.op("dve", lambda e, o=o, c0=c0, n=n, cf=cf, ca_t=ca_t, ha_t=ha_t: e.tensor_tensor(
                            out=cf[:, o:o + n], in0=ca_t[:, c0:c0 + n], in1=ha_t[:, c0:c0 + n], op=ALU.mult),
                            reads=[ha_r, ca_r], writes=[cfr])
                    masked_prefix(cf, cfr, HA, j * HA, l)
                    cva, cvar = zev.next()
                    wb = _off[("caw", l)] + j * 3
                    for si, (kind, c0, n) in enumerate(segs):
                        o = offs[si] - HA
                        P.op("dve", lambda e, o=o, c0=c0, n=n, cf=cf, cva=cva, wb=wb: e.tensor_scalar(
                            out=cva[:, c0:c0 + n], in0=cf[:, o:o + n], scalar1=prm[:, wb:wb + 1], scalar2=None, op0=ALU.mult),
                            reads=[cfr, "prm"], writes=[cvar])
                        for k in range(1, 3):
                            P.op("dve", lambda e, o=o, c0=c0, n=n, cf=cf, cva=cva, wb=wb, k=k: e.scalar_tensor_tensor(
                                out=cva[:, c0:c0 + n], in0=cf[:, o + k:o + k + n], scalar=prm[:, wb + k:wb + k + 1],
                                in1=cva[:, c0:c0 + n], op0=ALU.mult, op1=ALU.add),
                                reads=[cfr, "prm"], writes=[cvar])
                    carry_update(cf, cfr, HA, j * HA, l)
                    held["cva"] = (cva, cvar)
                elif typ == "ba":
                    ba_t, ba_r = held["ba"]
                    cva, cvar = held["cva"]
                    P.op("dve", lambda e, j=j, ba_t=ba_t, cva=cva: e.tensor_tensor(out=ycat[:, j, :], in0=ba_t[:], in1=cva[:], op=ALU.mult),
                         reads=[ba_r, cvar], writes=[("yc", j)])
                elif typ == "up":
                    g = j
                    w = 2 ** (g + 1)
                    uf, ufr = padb.next()
                    offs = seg_off(p, HC)
                    for (s, lo, hi, si) in pcs:
                        o = offs[si] + (lo - segs[si][1])
                        P.op("act", lambda e, uf=uf, o=o, lo=lo, hi=hi, b=banks[s], s=s: e.activation(
                            out=uf[:, o:o + (hi - lo)], in_=pb[b][:, lo - s * SUB:hi - s * SUB], func=AF.Identity),
                            writes=[("pb", banks[s]), ufr])
                    masked_prefix(uf, ufr, HC, 192 + g * HC, l)
                    dt_, dtr = dbf.next()
                    for si, (kind, c0, n) in enumerate(segs):
                        o = offs[si]
                        a = o - HC
                        end = o + n
                        src, srcr = uf, ufr
                        for m in range(g + 1):
                            sh = 2 ** m
                            lo_i = a + 2 ** (m + 1) - 1
                            dst, dstr = ptmp.next()
                            P.op("dve", lambda e, src=src, dst=dst, lo_i=lo_i, end=end, sh=sh: e.tensor_tensor(
                                out=dst[:, lo_i:end], in0=src[:, lo_i:end], in1=src[:, lo_i - sh:end - sh], op=ALU.add),
                                reads=[srcr], writes=[dstr])
                            src, srcr = dst, dstr
                        P.op("dve", lambda e, src=src, o=o, n=n, c0=c0, uf=uf, dt_=dt_, w=w: e.scalar_tensor_tensor(
                            out=dt_[:, c0:c0 + n], in0=src[:, o:o + n], scalar=1.0 / w, in1=uf[:, o:o + n],
                            op0=ALU.mult, op1=ALU.subtract),
                            reads=[srcr, ufr], writes=[dtr])
                        if p == 0 and kind == 0:
                            oo = o + HALO
                            cc = c0 + HALO
                            ib = _off["icnt"] + g * 16
                            tmpt, tmpr = ptmp.next()
                            P.op("dve", lambda e, src=src, oo=oo, ib=ib, tmpt=tmpt: e.tensor_tensor(
                                out=tmpt[:, 0:16], in0=src[:, oo:oo + 16], in1=prm[:, ib:ib + 16], op=ALU.mult),
                                reads=[srcr, "prm"], writes=[tmpr])
                            P.op("dve", lambda e, oo=oo, cc=cc, uf=uf, tmpt=tmpt, dt_=dt_: e.tensor_tensor(
                                out=dt_[:, cc:cc + 16], in0=tmpt[:, 0:16], in1=uf[:, oo:oo + 16], op=ALU.subtract),
                                reads=[tmpr, ufr], writes=[dtr])
                    carry_update(uf, ufr, HC, 192 + g * HC, l)
                    def pool_mm(g=g, dt_=dt_, dtr=dtr):
                        for s in range(2):
                            b = xB.next()[0]
                            cs = slice(s * SUB, (s + 1) * SUB)
                            P.op("pe", lambda e, b=b, cs=cs: e.matmul(pb[b][:, 0:SUB], lhsT=pw_sb[:, l * 4 + g, :], rhs=dt_[:, cs],
                                                                      start=True, stop=True),
                                 reads=["pw", dtr], writes=[("pb", b)])
                            P.op("act", lambda e, b=b, cs=cs: e.activation(out=ycat[:, 12 + g, cs], in_=pb[b][:, 0:SUB], func=AF.Identity,
                                                                           scale=pcol(("psc", l), g)),
                                 reads=["prm"], writes=[("pb", b), ("yc", 12 + g)])
                    pend_pe.append(pool_mm)
            while pend_pe:
                pend_pe.pop(0)()
            act_preload(AF.Sqrt)
            bs = (xB.next()[0], xB.next()[0])
            ms = (4, 5)
            for c in range(6):
                sq, sqr = sqb.next()
                P.op("act", lambda e, c=c, sq=sq: e.activation(out=sq[:], in_=convb[:, c, :], func=AF.Square),
                     reads=[("cb", c)], writes=[sqr])
                for s in range(2):
                    cs = slice(s * SUB, (s + 1) * SUB)
                    P.op("pe", lambda e, c=c, s=s, sq=sq, cs=cs: e.matmul(pb[bs[s]][:, 0:SUB], lhsT=ones[:], rhs=sq[:, cs],
                                                                          start=(c == 0), stop=(c == 5)),
                         reads=["ones", sqr], writes=[("pb", bs[s])])
                    P.op("pe", lambda e, c=c, s=s, cs=cs: e.matmul(pb[ms[s]][:, 0:SUB], lhsT=ones[:], rhs=convb[:, c, cs],
                                                                   start=(c == 0), stop=(c == 5)),
                         reads=["ones", ("cb", c)], writes=[("pb", ms[s])])
            ln_steps = []

            def ln_post():
                for s in range(2):
                    cs = slice(s * SUB, (s + 1) * SUB)
                    P.op("act", lambda e, s=s, cs=cs: e.activation(out=lnm[:, cs], in_=pb[ms[s]][:, 0:SUB], func=AF.Identity, scale=1.0 / 768),
                         writes=[("pb", ms[s]), ("lnm", s)])
                    P.op("dve", lambda e, s=s, cs=cs: e.tensor_tensor(out=lnt[:, cs], in0=lnm[:, cs], in1=lnm[:, cs], op=ALU.mult),
                         reads=[("lnm", s)], writes=[("lnt", s)])
                    P.op("dve", lambda e, s=s, cs=cs: e.scalar_tensor_tensor(out=lnt[:, cs], in0=pb[bs[s]][:, 0:SUB], scalar=1.0 / 768,
                                                                            in1=lnt[:, cs], op0=ALU.mult, op1=ALU.subtract),
                         writes=[("pb", bs[s]), ("lnt", s)])
                    P.op("act", lambda e, s=s, cs=cs: e.activation(out=lnr[:, cs], in_=lnt[:, cs], func=AF.Sqrt, scale=1.0, bias=epst[:, 0:1]),
                         reads=["epst", ("lnt", s)], writes=[("lnr", s)])
                    P.op("dve", lambda e, s=s, cs=cs: e.reciprocal(out=lnr[:, cs], in_=lnr[:, cs]), writes=[("lnr", s)])
                act_preload(AF.Silu)

            def ln_norm(j):
                P.op("dve", lambda e, j=j: e.tensor_tensor(out=convb[:, j, :], in0=convb[:, j, :], in1=lnm[:], op=ALU.subtract),
                     reads=[("lnm", 0), ("lnm", 1)], writes=[("cb", j)])
                P.op("dve", lambda e, j=j: e.tensor_tensor(out=convb[:, j, :], in0=convb[:, j, :], in1=lnr[:], op=ALU.mult),
                     reads=[("lnr", 0), ("lnr", 1)], writes=[("cb", j)])
                P.op("act", lambda e, j=j: e.activation(out=ycat[:, 6 + j, :], in_=convb[:, j, :], func=AF.Silu,
                                                       scale=pcol(("lng", l), j), bias=pcol(("lnb", l), j)),
                     reads=[("cb", j), "prm"], writes=[("yc", 6 + j)])

            ln_steps.append(ln_post)
            for j in range(6):
                ln_steps.append(lambda j=j: ln_norm(j))
            ln_steps.append(lambda: act_preload(AF.Sqrt))
            KA = [0, 1, 2, 3, 4, 5, 12, 13, 14, 15]
            KBL = [6, 7, 8, 9, 10, 11]
            tile_i = 0
            for hb in range(8):
                h = hctr[0] % 4
                hctr[0] += 1
                P.dma("pool", lambda e, h=h, hb=hb: e.dma_start(out=halves[h][:, 0:6, :], in_=wov[:, 0:6, hb * 256:(hb + 1) * 256]),
                      ("h", h), writes=[("h", h)])
                P.dma("pool", lambda e, h=h, hb=hb: e.dma_start(out=halves[h][:, 6:10, :], in_=wov[:, 12:16, hb * 256:(hb + 1) * 256]),
                      ("h", h), writes=[("h", h)])
                for t in range(2):
                    d = hb * 2 + t
                    for s in range(2):
                        b = allB.next()[0]
                        cs = slice(s * SUB, (s + 1) * SUB)
                        for i, kc in enumerate(KA):
                            P.op("pe", lambda e, i=i, kc=kc, h=h, t=t, cs=cs, b=b: e.matmul(
                                pb[b][:, 0:SUB], lhsT=halves[h][:, i, t * 128:(t + 1) * 128], rhs=ycat[:, kc, cs],
                                start=(i == 0), stop=(i == len(KA) - 1)),
                                reads=[("h", h), ("yc", kc)], writes=[("pb", b)])
                        P.op("dve", lambda e, d=d, cs=cs, b=b: e.tensor_tensor(out=resid[:, d, cs], in0=pb[b][:, 0:SUB],
                                                                              in1=resid[:, d, cs], op=ALU.add),
                             writes=[("pb", b), ("r", d, s)])
                        tile_i += 1
                        if tile_i % 4 == 2 and ln_steps:
                            ln_steps.pop(0)()
            while ln_steps:
                ln_steps.pop(0)()
            for hb in range(8):
                h = hctr[0] % 4
                hctr[0] += 1
                P.dma("pool", lambda e, h=h, hb=hb: e.dma_start(out=halves[h][:, 0:6, :], in_=wov[:, 6:12, hb * 256:(hb + 1) * 256]),
                      ("h", h), writes=[("h", h)])
                for t in range(2):
                    d = hb * 2 + t
                    for s in range(2):
                        b = tailB.next()[0]
                        cs = slice(s * SUB, (s + 1) * SUB)
                        for i, kc in enumerate(KBL):
                            P.op("pe", lambda e, i=i, kc=kc, h=h, t=t, cs=cs, b=b: e.matmul(
                                pb[b][:, 0:SUB], lhsT=halves[h][:, i, t * 128:(t + 1) * 128], rhs=ycat[:, kc, cs],
                                start=(i == 0), stop=(i == len(KBL) - 1)),
                                reads=[("h", h), ("yc", kc)], writes=[("pb", b)])
                        P.op("dve", lambda e, d=d, cs=cs, b=b: e.tensor_tensor(out=resid[:, d, cs], in0=pb[b][:, 0:SUB],
                                                                              in1=resid[:, d, cs], op=ALU.add),
                             writes=[("pb", b), ("r", d, s)])
                    rms_sq(d, lag=2)

        for p in range(DBG.get("passes", NP)):
            load_x(p)
            for kc in range(KC):
                rms_sq(kc)
            for l in range(DBG.get("layers", L)):
                if DBG.get("ffn1", True):
                    rmsnorm(("n1", l), AF.Silu)
                    ffn(l, w_d["ffn1_wg"], w_d["ffn1_wu"], w_d["ffn1_wd"])
                if DBG.get("mixer", True):
                    rmsnorm(("nm", l), AF.Sigmoid)
                    mixer(p, l)
                    if DBG.get("dump") and (p, l) == tuple(DBG["dump"]):
                        dbg_d = nc.dram_tensor("dbg", [128, KC, TT], F32, kind="ExternalOutput").ap()
                        P.dma("pool", lambda e: e.dma_start(out=dbg_d, in_=ycat[:]), "dbg", reads=[("yc", k) for k in range(KC)])
                if DBG.get("ffn2", True):
                    rmsnorm(("n2", l), AF.Silu)
                    ffn(l, w_d["ffn2_wg"], w_d["ffn2_wu"], w_d["ffn2_wd"])
            final_norm_store(p, p == NP - 1)
        P.dma("sp", lambda e: e.dma_start(out=nb_d, in_=car[:]), "nb", reads=["car"])
        P.emit(nc, final_waits=[("y", "sqb", i) for i in range(3)]
               + [("y", "zev", i) for i in range(6)] + ["nb", "dbg"])
    return nc


def _fm(v, nchunk):
    return np.ascontiguousarray(np.asarray(v, np.float32).reshape(nchunk, 128).T)


def _pack_prm(inp, c):
    prm = np.zeros((128, NPRM), np.float32)

    def put(name, arr):
        arr = np.asarray(arr, np.float32).reshape(128, -1)
        prm[:, _off[name]:_off[name] + arr.shape[1]] = arr

    for l in range(L):
        put(("n1", l), _fm(inp["ffn1_norm"][l], KC))
        put(("nm", l), _fm(inp["mix_norm"][l], KC))
        put(("n2", l), _fm(inp["ffn2_norm"][l], KC))
        put(("caw", l), np.asarray(inp["conv_a_w"][l]).reshape(3, 6, 128).transpose(2, 1, 0))
        put(("cbw", l), np.asarray(inp["conv_b_w"][l]).reshape(31, 6, 128).transpose(2, 1, 0))
        put(("cbb", l), _fm(inp["conv_b_bias"][l], 6))
        put(("lng", l), _fm(inp["ln_b_gain"][l], 6))
        put(("lnb", l), _fm(inp["ln_b_bias"][l], 6))
        put(("psc", l), _fm(inp["pool_scale"][l], 4))
    put("nf", _fm(inp["final_norm"], KC))
    q = c % 4
    prm[:, _off["hmask"]] = 0.0 if q == 0 else 1.0
    ic = np.zeros((4, 16), np.float32)
    for g in range(4):
        w = 2 ** (g + 1)
        for i in range(16):
            ic[g, i] = (1.0 / min(i + 1, w)) if q == 0 else (1.0 / w)
    prm[:, _off["icnt"]:_off["icnt"] + 64] = ic.reshape(1, 64)
    cache = np.zeros((128, L, CARW), np.float32)
    for l in range(L):
        cache[:, l, 0:12] = np.asarray(inp["cache_conv_a"][l, c]).reshape(HA, 6, 128).transpose(2, 1, 0).reshape(128, 12)
        cache[:, l, 12:192] = np.asarray(inp["cache_conv_b"][l, c]).reshape(HB, 6, 128).transpose(2, 1, 0).reshape(128, 180)
        cache[:, l, 192:252] = np.asarray(inp["cache_pool"][l, c]).reshape(HC, 4, 128).transpose(2, 1, 0).reshape(128, 60)
    prm[:, _off["cache"]:_off["cache"] + L * CARW] = cache.reshape(128, L * CARW)
    return prm


def kernel(**inputs):
    inp = {k: np.asarray(v) for k, v in inputs.items()}
    xp = inp["x_prompt"].astype(np.float32, copy=False)
    xs = inp["x_sample"].astype(np.float32, copy=False)
    B, S, _ = xp.shape
    QL = S // 4
    wnames = ("ffn1_wg", "ffn1_wu", "ffn1_wd", "w_in", "pool_w", "w_out", "ffn2_wg", "ffn2_wu", "ffn2_wd")
    wts = {k: np.ascontiguousarray(inp[k], dtype=np.float32) for k in wnames}
    in_maps = []
    for c in range(NCORES):
        b, q = c // 4, c % 4
        halo = np.zeros((HALO, D), np.float32) if q == 0 else xp[b, q * QL - HALO:q * QL]
        toks = np.concatenate([xs[c], halo, xp[b, q * QL:(q + 1) * QL]], axis=0)
        xT = np.ascontiguousarray(toks.reshape(NP, TT, KC, 128).transpose(3, 0, 2, 1))
        m = {"xT": xT, "prm": _pack_prm(inp, c)}
        m.update(wts)
        in_maps.append(m)
    nc = build_nc()
    res = run_bass_kernel_spmd(nc, in_maps, core_ids=list(range(NCORES)))
    y_prompt = np.zeros((B, S, D), np.float32)
    y_sample = np.zeros(xs.shape, np.float32)
    na_p = np.zeros((L, B, HA, 768), np.float32)
    nb_p = np.zeros((L, B, HB, 768), np.float32)
    np_p = np.zeros((L, B, HC, 512), np.float32)
    na_s = np.zeros((L, NCORES, HA, 768), np.float32)
    nb_s = np.zeros((L, NCORES, HB, 768), np.float32)
    np_s = np.zeros((L, NCORES, HC, 512), np.float32)
    for c in range(NCORES):
        b, q = c // 4, c % 4
        r = res.results[c]
        toks = np.asarray(r["yT"]).transpose(1, 3, 2, 0).reshape(NP * TT, D)
        y_sample[c] = toks[0:SAMP]
        y_prompt[b, q * QL:(q + 1) * QL] = toks[SAMP + HALO:]
        nbuf = np.asarray(r["nbuf"]).reshape(128, 2, L, CARW)
        for l in range(L):
            for kind, (da, db, dp, idx) in ((1, (na_s, nb_s, np_s, c)), (0, (na_p, nb_p, np_p, b))):
                if kind == 0 and q != 3:
                    continue
                v = nbuf[:, kind, l]
                da[l, idx] = v[:, 0:12].reshape(128, 6, HA).transpose(2, 1, 0).reshape(HA, 768)
                db[l, idx] = v[:, 12:192].reshape(128, 6, HB).transpose(2, 1, 0).reshape(HB, 768)
                dp[l, idx] = v[:, 192:252].reshape(128, 4, HC).transpose(2, 1, 0).reshape(HC, 512)
    return (y_prompt, y_sample, na_p, nb_p, np_p, na_s, nb_s, np_s)
```
